# Optimizing a Trainium2 kernel written in Bass

```python
import jax, jax.numpy as jnp
from jax import lax
import numpy as np

D_MODEL = 1024
BATCH = 8
SEQ = 2048
DEPTH = 4
DEC_BATCH = 128
DEC_SEQ = 8
PAST_LEN = 16384
PAGE_SIZE = 128

N_MIXERS = 2
N_A = (DEPTH + 1) // 2
N_B = DEPTH // 2
CONV_A = 3
CONV_B = 31
D_FF = ((8 * D_MODEL // 3 + 255) // 256) * 256
PLE_DIM = 256
RMS_EPS = 1e-6
LN_EPS = 1e-5

kernel_name = "hybrid_shortconv_conformer_decoder_step"


def rms_norm(x, g):
    xf = x.astype(jnp.float32)
    y = xf * lax.rsqrt(jnp.mean(xf * xf, axis=-1, keepdims=True) + RMS_EPS)
    return (y * g.astype(jnp.float32)).astype(x.dtype)


def layer_norm(x, g, b):
    xf = x.astype(jnp.float32)
    mu = jnp.mean(xf, axis=-1, keepdims=True)
    xc = xf - mu
    var = jnp.mean(xc * xc, axis=-1, keepdims=True)
    y = xc * lax.rsqrt(var + LN_EPS) * g.astype(jnp.float32) + b.astype(jnp.float32)
    return y.astype(x.dtype)


def causal_dwconv(u, past, w):
    k = w.shape[0]
    c = u.shape[-1]
    full = jnp.concatenate([past.astype(u.dtype), u], axis=1)
    y = lax.conv_general_dilated(
        full, w[:, None, :].astype(u.dtype), window_strides=(1,), padding="VALID",
        dimension_numbers=("NWC", "WIO", "NWC"), feature_group_count=c)
    return y, full[:, full.shape[1] - (k - 1):, :]


def short_conv_mixer(x, past, w_in, w_conv, w_out):
    bch = x @ w_in
    b, c, h = jnp.split(bch, 3, axis=-1)
    y, new_past = causal_dwconv(c * h, past, w_conv)
    return (b * y) @ w_out, new_past


def conformer_conv_mixer(x, past, w_pw1, b_pw1, w_dw, b_dw, ln_g, ln_b, w_pw2, b_pw2):
    a, g = jnp.split(x @ w_pw1 + b_pw1, 2, axis=-1)
    u = a * jax.nn.sigmoid(g)
    y, new_past = causal_dwconv(u, past, w_dw)
    y = jax.nn.silu(layer_norm(y + b_dw, ln_g, ln_b))
    return y @ w_pw2 + b_pw2, new_past


def swiglu_ffn(x, w_gu, w_down):
    g, u = jnp.split(x @ w_gu, 2, axis=-1)
    return (jax.nn.silu(g) * u) @ w_down


def trunk(x, p, st_a, st_b,
          g_mix, g_ffn, g_ple, g_final,
          a_w_in, a_conv, a_w_out,
          b_w_pw1, b_b_pw1, b_conv, b_b_conv, b_ln_g, b_ln_b, b_w_pw2, b_b_pw2,
          ffn_w_gu, ffn_w_down, ple_w_gate, ple_w_proj):
    h = x
    new_a = []
    new_b = []
    for i in range(DEPTH):
        j = i // N_MIXERS
        xn = rms_norm(h, g_mix[i])
        if i % N_MIXERS == 0:
            mix, s = short_conv_mixer(xn, st_a[j], a_w_in[j], a_conv[j], a_w_out[j])
            new_a.append(s)
        else:
            mix, s = conformer_conv_mixer(xn, st_b[j], b_w_pw1[j], b_b_pw1[j], b_conv[j],
                                          b_b_conv[j], b_ln_g[j], b_ln_b[j],
                                          b_w_pw2[j], b_b_pw2[j])
            new_b.append(s)
        h = h + mix
        h = h + swiglu_ffn(rms_norm(h, g_ffn[i]), ffn_w_gu[i], ffn_w_down[i])
        gate = jax.nn.sigmoid(rms_norm(h, g_ple[i]) @ ple_w_gate[i])
        h = h + gate * (p[i] @ ple_w_proj[i])
    return rms_norm(h, g_final), jnp.stack(new_a), jnp.stack(new_b)


def setup_inputs(seed: int = 0) -> dict:
    key = jax.random.key(seed)
    ks = iter(jax.random.split(key, 40))
    f32 = jnp.float32
    D = D_MODEL

    def nrm(shape, scale):
        return jax.random.normal(next(ks), shape, f32) * scale

    def gain(shape):
        return 1.0 + nrm(shape, 0.01)

    return {
        "x_prompt": nrm((BATCH, SEQ, D), 1.0),
        "x_sample": nrm((DEC_BATCH, DEC_SEQ, D), 1.0),
        "p_prompt": nrm((DEPTH, BATCH, SEQ, PLE_DIM), 1.0),
        "p_sample": nrm((DEPTH, DEC_BATCH, DEC_SEQ, PLE_DIM), 1.0),
        "state_conv_a": nrm((N_A, DEC_BATCH, CONV_A - 1, D), 1.0),
        "state_conv_b": nrm((N_B, DEC_BATCH, CONV_B - 1, D), 0.5),
        "g_mix": gain((DEPTH, D)),
        "g_ffn": gain((DEPTH, D)),
        "g_ple": gain((DEPTH, D)),
        "g_final": gain((D,)),
        "a_w_in": nrm((N_A, D, 3 * D), D ** -0.5),
        "a_conv": nrm((N_A, CONV_A, D), CONV_A ** -0.5),
        "a_w_out": nrm((N_A, D, D), D ** -0.5),
        "b_w_pw1": nrm((N_B, D, 2 * D), D ** -0.5),
        "b_b_pw1": nrm((N_B, 2 * D), 0.01),
        "b_conv": nrm((N_B, CONV_B, D), CONV_B ** -0.5),
        "b_b_conv": nrm((N_B, D), 0.01),
        "b_ln_g": gain((N_B, D)),
        "b_ln_b": nrm((N_B, D), 0.01),
        "b_w_pw2": nrm((N_B, D, D), D ** -0.5),
        "b_b_pw2": nrm((N_B, D), 0.01),
        "ffn_w_gu": nrm((DEPTH, D, 2 * D_FF), D ** -0.5),
        "ffn_w_down": nrm((DEPTH, D_FF, D), D_FF ** -0.5),
        "ple_w_gate": nrm((DEPTH, D, D), D ** -0.5),
        "ple_w_proj": nrm((DEPTH, PLE_DIM, D), PLE_DIM ** -0.5),
    }


def reference(x_prompt, x_sample, p_prompt, p_sample, state_conv_a, state_conv_b,
              g_mix, g_ffn, g_ple, g_final,
              a_w_in, a_conv, a_w_out,
              b_w_pw1, b_b_pw1, b_conv, b_b_conv, b_ln_g, b_ln_b, b_w_pw2, b_b_pw2,
              ffn_w_gu, ffn_w_down, ple_w_gate, ple_w_proj):
    weights = (g_mix, g_ffn, g_ple, g_final,
               a_w_in, a_conv, a_w_out,
               b_w_pw1, b_b_pw1, b_conv, b_b_conv, b_ln_g, b_ln_b, b_w_pw2, b_b_pw2,
               ffn_w_gu, ffn_w_down, ple_w_gate, ple_w_proj)
    bp = x_prompt.shape[0]
    zero_a = jnp.zeros((N_A, bp, CONV_A - 1, D_MODEL), x_prompt.dtype)
    zero_b = jnp.zeros((N_B, bp, CONV_B - 1, D_MODEL), x_prompt.dtype)
    y_prompt, sa_prompt, sb_prompt = trunk(x_prompt, p_prompt, zero_a, zero_b, *weights)
    y_sample, sa_sample, sb_sample = trunk(x_sample, p_sample, state_conv_a, state_conv_b, *weights)
    return (y_prompt, y_sample, sa_prompt, sa_sample, sb_prompt, sb_sample)
```

```python
import contextlib
import numpy as np
import concourse.bass as bass
import concourse.mybir as mybir
from concourse.bass_utils import run_bass_kernel_spmd

F32 = mybir.dt.float32
BF16 = mybir.dt.bfloat16
ALU = mybir.AluOpType
AF = mybir.ActivationFunctionType

D = 1024
KC = 8
DFF = 2816
FC = 22
PLE = 256
DEPTH = 4
NCORE = 8
NT = 1088
NPR = 1024
NSQ = 8
NSM = 64
NST = 2
RMS_EPS = 1e-6
LN_EPS = 1e-5
N_DVE_TAPS = 7

V_GMIX, V_GFFN, V_GPLE = 0, 4, 8
V_ACONV = 12
V_BPW1 = 18
V_BDW = 22
V_LNG = 24
V_LNB = 26
V_BPW2 = 28
V_BCONV = 30
V_HALF = 92
NV = 96

RING_ELEMS = 12288
RBLK = 256
N_WSEM = 12


def weight_plan():
    plan = []
    for l in range(DEPTH):
        j = l // 2
        if l % 2 == 0:
            for cp in range(4):
                plan.append(("a_w_in", j, ("c", l, cp), 0, KC, 1024 + 256 * cp, 256))
                plan.append(("a_w_in", j, ("h", l, cp), 0, KC, 2048 + 256 * cp, 256))
                plan.append(("a_w_in", j, ("b", l, cp), 0, KC, 256 * cp, 256))
            for cp in range(4):
                plan.append(("a_w_out", j, ("o", l, cp), 0, KC, 256 * cp, 256))
        else:
            for cp in range(4):
                plan.append(("b_w_pw1", j, ("a", l, cp), 0, KC, 256 * cp, 256))
                plan.append(("b_w_pw1", j, ("g", l, cp), 0, KC, 1024 + 256 * cp, 256))
            for cp in range(4):
                plan.append(("b_w_pw2", j, ("o", l, cp), 0, KC, 256 * cp, 256))
        for fp in range(FC // 2):
            plan.append(("ffn_w_gu", l, ("fg", l, fp), 0, KC, 256 * fp, 256))
            plan.append(("ffn_w_gu", l, ("fu", l, fp), 0, KC, DFF + 256 * fp, 256))
        for n in range(KC):
            plan.append(("ffn_w_down", l, ("fd", l, n), 0, FC, 128 * n, 128))
        plan.append(("ple_w_proj", l, ("pp", l, 0), 0, 2, 0, 1024))
        for cp in range(4):
            plan.append(("ple_w_gate", l, ("pg", l, cp), 0, KC, 256 * cp, 256))
    return plan


def pack_weights(inputs):
    plan = weight_plan()
    total = sum(p[4] * p[6] for p in plan)
    wall = np.empty((128, total), dtype=np.float32)
    off = 0
    for (name, idx, key, k0, kn, n0, nn) in plan:
        w = np.asarray(inputs[name][idx], dtype=np.float32)
        blk = w[k0 * 128:(k0 + kn) * 128, n0:n0 + nn].reshape(kn, 128, nn).transpose(1, 0, 2)
        wall[:, off:off + kn * nn] = blk.reshape(128, kn * nn)
        off += kn * nn
    return wall


def pack_vecs(inputs):
    rows = [None] * NV
    for i in range(4):
        rows[V_GMIX + i] = inputs["g_mix"][i]
        rows[V_GFFN + i] = inputs["g_ffn"][i]
        rows[V_GPLE + i] = inputs["g_ple"][i]
    for j in range(2):
        for k in range(3):
            rows[V_ACONV + j * 3 + k] = inputs["a_conv"][j][k]
        rows[V_BPW1 + j * 2 + 0] = inputs["b_b_pw1"][j][:D]
        rows[V_BPW1 + j * 2 + 1] = inputs["b_b_pw1"][j][D:]
        rows[V_BDW + j] = inputs["b_b_conv"][j]
        rows[V_LNG + j] = inputs["b_ln_g"][j]
        rows[V_LNB + j] = inputs["b_ln_b"][j]
        rows[V_BPW2 + j] = inputs["b_b_pw2"][j]
        for k in range(31):
            rows[V_BCONV + j * 31 + k] = inputs["b_conv"][j][k]
    for i in range(NV):
        if rows[i] is None:
            rows[i] = np.zeros(D, np.float32)
    v = np.stack([np.asarray(r, np.float32) for r in rows])
    return np.ascontiguousarray(v.reshape(NV, KC, 128).transpose(2, 0, 1))


class Op:
    __slots__ = ("eng", "fn", "dma", "dval", "deps_c", "deps_d", "signal", "sig")

    def __init__(self, eng, fn, dma):
        self.eng = eng
        self.fn = fn
        self.dma = dma
        self.dval = 0
        self.deps_c = {}
        self.deps_d = {}
        self.signal = False
        self.sig = 0


class Prog:
    ENGS = ("pe", "act", "dve", "pool", "sp")

    def __init__(self, nc):
        self.nc = nc
        self.ops = []
        self.lastw = {}
        self.readers = {}
        self.dcount = {}

    AUTO_READS = {"act": ("cst", "vecs", "vhalf"), "dve": ("vecs", "vhalf", "identb"), "pool": ("vecs",),
                  "pe": ("ident", "ones")}

    def add(self, eng, fn, reads=(), writes=(), dma=None):
        op = Op(eng, fn, dma)
        idx = len(self.ops)
        if dma is None:
            wset = set(writes)
            reads = list(reads) + [t for t in self.AUTO_READS.get(eng, ()) if t not in wset]

        def dep(i, raw):
            o = self.ops[i]
            if o.dma is not None:
                if o.dval > op.deps_d.get(o.dma, 0):
                    op.deps_d[o.dma] = o.dval
                return
            if o.eng == eng:
                if eng == "pe" or not raw:
                    return
            if i > op.deps_c.get(o.eng, -1):
                op.deps_c[o.eng] = i

        for t in reads:
            lw = self.lastw.get(t)
            if lw is not None:
                dep(lw, True)
        for t in writes:
            lw = self.lastw.get(t)
            if lw is not None:
                dep(lw, False)
            for r in self.readers.get(t, ()):
                dep(r, False)
        for t in reads:
            self.readers.setdefault(t, []).append(idx)
        for t in writes:
            self.lastw[t] = idx
            self.readers[t] = []
        if dma is not None:
            self.dcount[dma] = self.dcount.get(dma, 0) + 16
            op.dval = self.dcount[dma]
        self.ops.append(op)
        return idx

    def emit(self, stack):
        nc = self.nc
        for op in self.ops:
            for e, i in op.deps_c.items():
                self.ops[i].signal = True
        eng_h = {"pe": nc.tensor, "act": nc.scalar, "dve": nc.vector, "pool": nc.gpsimd, "sp": nc.sync}
        esem = {e: stack.enter_context(nc.semaphore("s_" + e)) for e in self.ENGS}
        dsem = {k: stack.enter_context(nc.semaphore("d_" + str(k))) for k in self.dcount}
        count = {e: 0 for e in self.ENGS}
        waited = {e: {} for e in self.ENGS}
        for op in self.ops:
            h = eng_h[op.eng]
            w = waited[op.eng]
            for e, i in op.deps_c.items():
                v = self.ops[i].sig
                assert v > 0
                if w.get(("c", e), 0) < v:
                    h.wait_ge(esem[e], v)
                    w[("c", e)] = v
            for k, v in op.deps_d.items():
                if w.get(("d", k), 0) < v:
                    h.wait_ge(dsem[k], v)
                    w[("d", k)] = v
            ins = op.fn(h)
            if op.dma is not None:
                ins.then_inc(dsem[op.dma], 16)
            elif op.signal:
                count[op.eng] += 1
                op.sig = count[op.eng]
                ins.then_inc(esem[op.eng], 1)
        h = nc.sync
        for k, v in self.dcount.items():
            if waited["sp"].get(("d", k), 0) < v:
                h.wait_ge(dsem[k], v)
        for e in self.ENGS:
            if e != "sp" and count[e] > 0 and waited["sp"].get(("c", e), 0) < count[e]:
                h.wait_ge(esem[e], count[e])


def build_program(depth=DEPTH, n_st=NST):
    nc = bass.Bass("TRN2", target_bir_lowering=False)
    plan = weight_plan()
    wtotal = sum(p[4] * p[6] for p in plan)

    def din(name, shape):
        return nc.dram_tensor(name, list(shape), F32, kind="ExternalInput").ap()

    def dout(name, shape):
        return nc.dram_tensor(name, list(shape), F32, kind="ExternalOutput").ap()

    xp = din("xp", (2048, D))
    xs = din("xs", (128, D))
    pp = din("pp", (DEPTH, 2048, PLE))
    ps_in = din("ps", (DEPTH, 128, PLE))
    sa_in = din("sa", (2, 16, 2, D))
    sb_in = din("sb", (2, 16, 30, D))
    vecs_in = din("vecs_in", (128, NV, KC))
    gfin_in = din("gfin_in", (128, D))
    wall = din("wall", (128, wtotal))
    yp = dout("yp", (2048, D))
    ys = dout("ys", (128, D))
    sap = dout("sap", (2, 2, D))
    sas = dout("sas", (2, 16, 2, D))
    sbp = dout("sbp", (2, 30, D))
    sbs = dout("sbs", (2, 16, 30, D))

    stack = contextlib.ExitStack()

    def sb(name, shape, dt=F32):
        return stack.enter_context(nc.sbuf_tensor(name, list(shape), dt))

    h = sb("h", (128, KC, NT))
    xn = sb("xn", (128, KC, NT), BF16)
    big = sb("big", (128, FC * NT), BF16)
    wring = sb("wring", (128, RING_ELEMS), BF16)
    ub = [sb("ub%d" % i, (128, 1358)) for i in range(2)]
    ub16 = [sb("ub16_%d" % i, (128, 1358), BF16) for i in range(2)]
    diag = [sb("diag%d" % i, (128, 31, 128), BF16) for i in range(2)]
    identb = sb("identb", (128, 128), BF16)
    T = [sb("T%d" % i, (128, NT)) for i in range(3)]
    sq = [sb("sq%d" % i, (128, NT), BF16) for i in range(3)]
    S = [sb("S%d" % i, (128, NT)) for i in range(3)]
    pT = sb("pT", (128, 2, NT), BF16)
    stg = [sb("stg%d" % i, (128, D)) for i in range(2)]
    stgp = sb("stgp", (128, PLE))
    stA_in = sb("stA_in", (128, KC, 16))
    haloA = sb("haloA", (128, 2, KC, 2))
    haloB = sb("haloB", (128, 2, KC, 30))
    vecs = sb("vecs", (128, NV, KC))
    ident = sb("ident", (128, 128))
    onesD = sb("onesD", (128, 128), BF16)
    cst = sb("cst", (128, 4))
    ssq = sb("ssq", (128, 16))
    psum = stack.enter_context(nc.psum_tensor("psumt", [128, 4096], F32))

    big32 = big[:].bitcast(F32) if hasattr(big[:], "bitcast") else None
    assert big32 is not None
    stB_view = big32[:, 8 * NT: 8 * NT + KC * 240].rearrange("p (c r) -> p c r", r=240)
    stcol = big32[:, 10 * NT: 10 * NT + KC * 94].rearrange("p (c r) -> p c r", r=94)
    gfin = big32[:, 0:D]

    xn32 = xn[:].rearrange("p c t -> p (c t)").bitcast(F32)

    def act_slot(f):
        return big[:, f * NT:(f + 1) * NT]

    def y_slot(j):
        return big32[:, j * NT:(j + 1) * NT]

    def tk_big(*fs):
        return [("big", f) for f in fs]

    P = Prog(nc)
    ZB = cst[:, 0:1]

    gstate = {"g": 0, "gs": 0, "stg": 0, "T": 0, "sq": 0}

    held = {"p": None, "s": None}

    def G(sample=True):
        gi = gstate["g"]
        if gi == held["p"]:
            gi = (gi + 1) % 3
        gstate["g"] = (gi + 1) % 3
        si = None
        if sample:
            si = gstate["gs"]
            if si == held["s"]:
                si = (si + 1) % 2
            gstate["gs"] = (si + 1) % 2
        return (gi, si)

    def pt(g):
        return [("psp", g[0])] + ([("pss", g[1])] if g[1] is not None else [])

    win = {"c0": 0, "n": NT}

    def W(ap):
        return ap[:, win["c0"]:win["c0"] + win["n"]]

    def both(f):
        def fn(e):
            win.update(c0=0, n=NPR)
            f(e)
            win.update(c0=NPR, n=NSM)
            r = f(e)
            win.update(c0=0, n=NT)
            return r
        return fn

    MMP = [(0, 512), (512, 512), (1024, 64)]

    def next_stg():
        i = gstate["stg"]
        gstate["stg"] = (i + 1) % 2
        return i

    def next_T():
        i = gstate["T"]
        gstate["T"] = (i + 1) % 3
        return i

    def pg(g, c0=None, n=None):
        if c0 is None:
            c0, n = win["c0"], win["n"]
        if c0 + n <= NPR:
            b = g[0] * 1024 + c0
        else:
            assert c0 >= NPR and g[1] is not None, (g, c0, n)
            b = 3072 + g[1] * 512 + (c0 - NPR)
        return psum[:, b:b + n]

    ring = {"off": 0, "next": 0, "live": [], "resident": {}}
    wplan = [(s,) + p for s in range(n_st) for p in plan if p[2][1] < depth]
    NP_ = len(wplan)
    wsrc_off = {}
    o = 0
    for p in plan:
        wsrc_off[p[2]] = o
        o += p[4] * p[6]

    def ring_tokens(off, size):
        return [("wr", b) for b in range(off // RBLK, (off + size - 1) // RBLK + 1)]

    def prefetch():
        while ring["next"] < NP_:
            (s, name, idx, key, k0, kn, n0, nn) = wplan[ring["next"]]
            size = kn * nn
            live = ring["live"]
            off = ring["off"]
            if off + size > RING_ELEMS:
                off = 0
            ok = True
            for (lo, ls, _) in live:
                if lo < off + size and off < lo + ls:
                    ok = False
                    break
            if not ok:
                return
            q = ring["next"]
            ring["next"] += 1
            ring["off"] = off + size
            live.append((off, size, (s, key)))
            ring["resident"][(s, key)] = (off, kn, nn)
            so = wsrc_off[key]
            P.add("pool",
                  (lambda e, off=off, size=size, so=so:
                   e.dma_start(out=wring[:, off:off + size], in_=wall[:, so:so + size])),
                  writes=ring_tokens(off, size), dma="w%d" % (q % N_WSEM))

    def wpiece(s, key):
        off, kn, nn = ring["resident"][(s, key)]
        return off, kn, nn

    def wfree(s, key):
        ring["live"] = [x for x in ring["live"] if x[2] != (s, key)]
        del ring["resident"][(s, key)]
        prefetch()

    def w_lhsT(s, key, k, c0):
        off, kn, nn = wpiece(s, key)
        a = off + k * nn + c0
        return wring[:, a:a + 128]

    def w_tokens(s, key):
        off, kn, nn = wpiece(s, key)
        return ring_tokens(off, kn * nn)

    def mm_group(g, klist, lhsT_fn, rhs_fn, reads_fn, wtok):
        klist = list(klist)
        nk = len(klist)
        lts = [lhsT_fn(k) for k in klist]
        rs = [rhs_fn(k) for k in klist]
        allreads = []
        for i, k in enumerate(klist):
            rd = list(reads_fn(k))
            allreads += rd

            def fn(e, i=i):
                ins = None
                for (c0, n) in MMP[0:2]:
                    ins = e.matmul(pg(g, c0, n), lts[i], rs[i][:, c0:c0 + n], start=(i == 0), stop=(i == nk - 1))
                return ins
            P.add("pe", fn, reads=rd + list(wtok), writes=[("psp", g[0])])

        def fns(e):
            ins = None
            (c0, n) = MMP[2]
            for i in range(nk):
                ins = e.matmul(pg(g, c0, n), lts[i], rs[i][:, c0:c0 + n], start=(i == 0), stop=(i == nk - 1))
            return ins
        P.add("pe", fns, reads=allreads + list(wtok), writes=[("pss", g[1])])

    def mm_multi(specs):
        prep = []
        for sp in specs:
            kl = list(sp["klist"])
            prep.append(dict(g=sp["g"], nk=len(kl), lts=[sp["lhsT_fn"](k) for k in kl],
                             rs=[sp["rhs_fn"](k) for k in kl], rds=[list(sp["reads_fn"](k)) for k in kl],
                             wtok=list(sp["wtok"])))
        def emit_k(p, i):
            def fn(e, i=i, p=p):
                ins = None
                for (c0, n) in MMP[0:2]:
                    ins = e.matmul(pg(p["g"], c0, n), p["lts"][i], p["rs"][i][:, c0:c0 + n], start=(i == 0),
                                   stop=(i == p["nk"] - 1))
                return ins
            P.add("pe", fn, reads=p["rds"][i] + p["wtok"], writes=[("psp", p["g"][0])])

        TAIL = 2
        SKEW = 2
        for i in range(max(p["nk"] for p in prep) + SKEW):
            for gi, p in enumerate(prep):
                ii = i - (SKEW if gi >= 2 else 0)
                if 0 <= ii < p["nk"] - TAIL:
                    emit_k(p, ii)
        for p in prep:
            for i in range(max(p["nk"] - TAIL, 0), p["nk"]):
                emit_k(p, i)
        ems = []
        for p in prep:
            def em(p=p):
                def fns(e):
                    ins = None
                    (c0, n) = MMP[2]
                    for i in range(p["nk"]):
                        ins = e.matmul(pg(p["g"], c0, n), p["lts"][i], p["rs"][i][:, c0:c0 + n], start=(i == 0),
                                       stop=(i == p["nk"] - 1))
                    return ins
                P.add("pe", fns, reads=[t for r in p["rds"] for t in r] + p["wtok"], writes=[("pss", p["g"][1])])
            ems.append(em)
        return ems

    def xn_spec(s_, key, co):
        return dict(g=G(), klist=range(KC), lhsT_fn=lambda k: w_lhsT(s_, key, k, co), rhs_fn=xn_rhs,
                    reads_fn=lambda k: [("xn", k)], wtok=w_tokens(s_, key))

    def xn_rhs(k):
        return xn[:, k, :]

    def tr_in(g, src, R, nchunk, blk0=0):
        def fn(e):
            ins = None
            for c in range(nchunk):
                ins = e.transpose(pg(g, (blk0 + c) * 128, R), src[0:R, c * 128:(c + 1) * 128], ident[0:R, 0:R])
            return ins
        return fn

    P.add("sp", lambda e: e.dma_start(out=vecs[:], in_=vecs_in[:, :, :]), writes=["vecs"], dma="c0")
    prefetch()

    P.add("pool", lambda e: e.memset(ident[:], 0.0), writes=["ident"])
    P.add("pool", lambda e: e.affine_select(out=ident[:], in_=ident[:], compare_op=ALU.not_equal, fill=1.0, base=0,
                                            pattern=[[-1, 128]], channel_multiplier=1),
          reads=["ident"], writes=["ident"])
    P.add("pool", lambda e: e.tensor_copy(identb[:], ident[:]), reads=["ident"], writes=["identb"])

    def setup_pool(e):
        e.memset(onesD[:], 1.0 / D)
        e.memset(cst[:, 0:1], 0.0)
        e.memset(cst[:, 1:2], RMS_EPS)
        e.memset(cst[:, 2:3], LN_EPS)
        e.memset(haloA[:], 0.0)
        return e.memset(haloB[:], 0.0)
    P.add("pool", setup_pool, writes=["ones", "cst", "haloA", "haloB"])
    P.add("dve", lambda e: e.tensor_scalar(vecs[:, V_HALF:V_HALF + 4, :], vecs[:, V_BPW1:V_BPW1 + 4, :], 0.5, None,
                                           ALU.mult),
          reads=["vecs"], writes=["vhalf"])

    def vec(i, c):
        return vecs[:, i, c:c + 1]

    XS0 = 2 * NT
    XTOK = tk_big(*range(4, 21))

    def prefetch_x(s):
        for hf in range(2):
            P.add("sp", lambda e, hf=hf: e.dma_start(
                out=big32[:, XS0 + hf * 4 * D:XS0 + (hf + 1) * 4 * D].rearrange("p (t d) -> p t d", d=D),
                in_=xp[s * NPR + hf * 512:s * NPR + (hf + 1) * 512, :].rearrange("(t p) d -> p t d", p=128)),
                writes=(XTOK if hf == 0 else []) + [("xst", 4 * hf + t_) for t_ in range(4)], dma="xl%d" % hf)
        P.add("sp", lambda e: e.dma_start(out=big32[0:NSM, XS0 + 8 * D:XS0 + 9 * D], in_=xs[s * NSM:(s + 1) * NSM, :]),
              writes=[("xst", 8)], dma="xl8")

    def load_x(s):
        tiles = [(128, i * 128) for i in range(8)] + [(NSM, NPR)]
        for i, (R, col0) in enumerate(tiles):
            src = big32[:, XS0 + i * D:XS0 + (i + 1) * D]
            g = G(False)
            P.add("pe", tr_in(g, src, R, KC), reads=XTOK + [("xst", i), "ident"], writes=pt(g))
            if i % 2 == 0:
                P.add("act", lambda e, g=g, R=R, col0=col0: e.activation(
                    out=h[:, :, col0:col0 + R],
                    in_=pg(g, 0, 1024).rearrange("p (c r) -> p c r", r=128)[:, :, 0:R],
                    func=AF.Identity, bias=ZB, scale=1.0),
                    reads=pt(g), writes=[("h", c) for c in range(KC)])
            else:
                P.add("dve", lambda e, g=g, R=R, col0=col0: e.tensor_copy(
                    h[:, :, col0:col0 + R], pg(g, 0, 1024).rearrange("p (c r) -> p c r", r=128)[:, :, 0:R]),
                    reads=pt(g), writes=[("h", c) for c in range(KC)])

    def load_p(l, s):
        groups = [(pp[l, s * NPR + gi * 512: s * NPR + (gi + 1) * 512, :], 4, 128, gi * 512) for gi in range(2)]
        groups.append((ps_in[l, s * NSM:(s + 1) * NSM, :], 1, NSM, NPR))
        staged = []
        for (src, nt, R, col0) in groups:
            if nt == 4:
                si = next_stg()
                buf, tok = stg[si], ("stg", si)
                P.add("sp", lambda e, src=src, buf=buf: e.dma_start(
                    out=buf[:, :].rearrange("p (t d) -> p t d", d=PLE),
                    in_=src.rearrange("(t p) d -> p t d", p=128)),
                    writes=[tok], dma="stg%d" % si)
            else:
                buf, tok = stgp, "stgp"
                P.add("sp", lambda e, src=src, buf=buf, R=R: e.dma_start(out=buf[0:R, 0:PLE], in_=src),
                      writes=[tok], dma="stgp")
            staged.append((buf, tok))
        for (src, nt, R, col0), (buf, tok) in zip(groups, staged):
            g = G(False)

            def fn(e, g=g, buf=buf, nt=nt, R=R):
                ins = None
                for t in range(nt):
                    for c in range(2):
                        ins = e.transpose(pg(g, (t * 2 + c) * 128, R),
                                          buf[0:R, t * PLE + c * 128: t * PLE + (c + 1) * 128],
                                          ident[0:R, 0:R])
                return ins
            P.add("pe", fn, reads=[tok, "ident"], writes=pt(g))

            def ev(e, g=g, nt=nt, R=R, col0=col0):
                ins = None
                for c in range(2):
                    src_v = pg(g, 0, nt * 256).rearrange("p (t c r) -> p t c r", c=2, r=128)[:, :, c, 0:R]
                    dst_v = pT[:, c, col0:col0 + nt * R].rearrange("p (t r) -> p t r", r=R)
                    ins = e.activation(out=dst_v, in_=src_v, func=AF.Identity, bias=ZB, scale=1.0)
                return ins
            P.add("act", ev, reads=pt(g), writes=["pT"])

    nstate = {"pending": None, "first": True, "on": True, "ps": 4, "pm": False, "hg": None}

    def postswitch():
        P.add("act", lambda e: e.activation(out=cst[:, 3:4], in_=cst[:, 0:1], func=AF.Tanh, bias=ZB, scale=1.0),
              reads=["cst"], writes=["cstdummy"])

    def preswitch():
        P.add("act", lambda e: e.activation(out=cst[:, 3:4], in_=cst[:, 1:2], func=AF.Ln, bias=ZB, scale=1.0),
              reads=["cst"], writes=["cstdummy"])

    def next_sq():
        qi = gstate["sq"]
        gstate["sq"] = (qi + 1) % len(sq)
        return qi

    def stats_mm(g, qi, start=True, stop=True):
        def fn(e):
            ins = None
            for (c0, n) in MMP:
                ins = e.matmul(pg(g, c0, n), onesD[:], sq[qi][:, c0:c0 + n], start=start, stop=stop)
            return ins
        P.add("pe", fn, reads=[("sq", qi), "ones"], writes=pt(g))

    def acc_into(ix, g, first):
        Sx = S[ix]
        if first:
            P.add("dve", both(lambda e: e.tensor_copy(W(Sx[:]), pg(g))), reads=pt(g), writes=[("S", ix)])
        else:
            P.add("dve", both(lambda e: e.tensor_tensor(out=W(Sx[:]), in0=pg(g), in1=W(Sx[:]), op=ALU.add)),
                  reads=pt(g) + [("S", ix)], writes=[("S", ix)])

    def norm_flush(final=False):
        if nstate["pending"] is not None:
            qi = nstate["pending"]
            nstate["pending"] = None
            if nstate["pm"]:
                if nstate["hg"] is None:
                    nstate["hg"] = G()
                    held["p"], held["s"] = nstate["hg"]
                stats_mm(nstate["hg"], qi, start=nstate["first"], stop=final)
            else:
                g = G()
                stats_mm(g, qi)
                acc_into(2, g, nstate["first"])
            nstate["first"] = False

    def norm_feed(c):
        if not nstate["on"]:
            return
        norm_flush()
        if c == nstate["ps"]:
            preswitch()
        qi = next_sq()
        P.add("act", lambda e, c=c, qi=qi: e.activation(out=sq[qi][:], in_=h[:, c, :], func=AF.Square,
                                                        bias=ZB, scale=1.0),
              reads=[("h", c)], writes=[("sq", qi)])
        nstate["pending"] = qi

    def rmsnorm(vidx):
        norm_flush(final=True)
        nstate["first"] = True
        if nstate["hg"] is not None:
            hg = nstate["hg"]
            P.add("act", both(lambda e, hg=hg: e.activation(out=W(S[0][:]), in_=pg(hg), func=AF.Ln, bias=cst[:, 1:2],
                                                            scale=1.0)),
                  reads=pt(hg) + ["cst"], writes=[("S", 0)])
            nstate["hg"] = None
            held["p"] = held["s"] = None
        else:
            P.add("act", lambda e: e.activation(out=S[0][:], in_=S[2][:], func=AF.Ln, bias=cst[:, 1:2], scale=1.0),
                  reads=[("S", 2), "cst"], writes=[("S", 0)])
        nstate["pm"] = False
        P.add("act", lambda e: e.activation(out=S[1][:], in_=S[0][:], func=AF.Exp, bias=ZB, scale=-0.5),
              reads=[("S", 0), "cst"], writes=[("S", 1)])
        postswitch()
        for c in range(KC):
            P.add("dve", lambda e, c=c: e.scalar_tensor_tensor(out=xn[:, c, :], in0=h[:, c, :], scalar=vec(vidx, c),
                                                               in1=S[1][:], op0=ALU.mult, op1=ALU.mult),
                  reads=[("h", c), ("S", 1), "vecs"], writes=[("xn", c)])

    def resid_add(n, g, bias_idx=None):
        if bias_idx is None:
            P.add("dve", both(lambda e, n=n, g=g: e.tensor_tensor(out=W(h[:, n, :]), in0=pg(g), in1=W(h[:, n, :]), op=ALU.add)),
                  reads=pt(g) + [("h", n)], writes=[("h", n)])
            norm_feed(n)
        else:
            P.add("dve", both(lambda e, n=n, g=g: e.scalar_tensor_tensor(out=W(h[:, n, :]), in0=pg(g), scalar=vec(bias_idx, n),
                                                                    in1=W(h[:, n, :]), op0=ALU.add, op1=ALU.add)),
                  reads=pt(g) + [("h", n), "vecs"], writes=[("h", n)])
            norm_feed(n)

    def out_proj(s, l, rhs_fn, rtok_fn, bias_idx=None):
        nstate["pm"] = nstate["on"]

        def spec(n):
            key = ("o", l, n // 2)
            return dict(g=G(), klist=range(KC), lhsT_fn=lambda k: w_lhsT(s, key, k, (n % 2) * 128), rhs_fn=rhs_fn,
                        reads_fn=rtok_fn, wtok=w_tokens(s, key))
        sp = [spec(0), spec(1), spec(2)]
        em = mm_multi(sp)
        em[0]()
        em[1]()
        wfree(s, ("o", l, 0))
        resid_add(0, sp[0]["g"], bias_idx)
        em[2]()
        resid_add(1, sp[1]["g"], bias_idx)
        resid_add(2, sp[2]["g"], bias_idx)
        for n in range(3, KC):
            key = ("o", l, n // 2)
            g = G()
            mm_group(g, range(KC), lambda k, key=key, n=n: w_lhsT(s, key, k, (n % 2) * 128), rhs_fn, rtok_fn,
                     w_tokens(s, key))
            if n % 2 == 1:
                wfree(s, key)
            resid_add(n, g, bias_idx)

    def state_out(s, ncols, dsts):
        g = G(False)

        def fn(e):
            ins = None
            for c in range(KC):
                ins = e.transpose(pg(g, c * 128, 128)[0:ncols, :], stcol[:, c, 0:ncols], ident[:, :])
            return ins
        P.add("pe", fn, reads=tk_big(20, 21) + ["ident"], writes=pt(g))
        si = next_stg()
        P.add("act", lambda e: e.activation(out=stg[si][0:ncols, :], in_=pg(g, 0, 1024)[0:ncols, :], func=AF.Identity,
                                            bias=cst[0:ncols, 0:1], scale=1.0),
              reads=pt(g) + ["cst"], writes=[("stg", si)])
        for (r0, r1, dst) in dsts:
            P.add("sp", lambda e, r0=r0, r1=r1, dst=dst: e.dma_start(out=dst, in_=stg[si][r0:r1, :]),
                  reads=[("stg", si)], dma="out%d" % si)

    def load_state(s, l):
        j = l // 2
        if l % 2 == 0:
            si = next_stg()
            P.add("sp", lambda e: e.dma_start(out=stg[si][0:16, :],
                                              in_=sa_in[j, s * NSQ:(s + 1) * NSQ, :, :].rearrange("q t d -> (q t) d")),
                  writes=[("stg", si)], dma="stg%d" % si)
            g0 = G(False)
            P.add("pe", tr_in(g0, stg[si], 16, KC), reads=[("stg", si), "ident"], writes=pt(g0))
            P.add("act", lambda e: e.activation(out=stA_in[:, :, :],
                                                in_=pg(g0, 0, 1024).rearrange("p (c r) -> p c r", r=128)[:, :, 0:16],
                                                func=AF.Identity, bias=ZB, scale=1.0),
                  reads=pt(g0), writes=["stA_in"])
        else:
            for (r0, R) in ((0, 128), (128, 112)):
                si = next_stg()
                P.add("sp", lambda e, si=si, r0=r0, R=R: e.dma_start(
                    out=stg[si][0:R, :],
                    in_=sb_in[j, s * NSQ:(s + 1) * NSQ, :, :].rearrange("q t d -> (q t) d")[r0:r0 + R, :]),
                    writes=[("stg", si)], dma="stg%d" % si)
                g0 = G(False)
                P.add("pe", tr_in(g0, stg[si], R, KC), reads=[("stg", si), "ident"], writes=pt(g0))
                P.add("act", lambda e, g0=g0, r0=r0, R=R: e.activation(
                    out=stB_view[:, :, r0:r0 + R],
                    in_=pg(g0, 0, 1024).rearrange("p (c r) -> p c r", r=128)[:, :, 0:R],
                    func=AF.Identity, bias=ZB, scale=1.0),
                    reads=pt(g0), writes=tk_big(16, 17, 18, 19))
            P.add("sp", lambda e: e.dma_start(out=sbs[j, s * NSQ:(s + 1) * NSQ, 0:22, :],
                                              in_=sb_in[j, s * NSQ:(s + 1) * NSQ, 8:30, :]), dma="dd")

    def mixer_A(s, l):
        j = l // 2
        rmsnorm(V_GMIX + l)
        for c in range(KC):
            cp = c // 2
            co = (c % 2) * 128
            cb = ub[c % 2]
            cbs = cb[:, 1026:1106].rearrange("p (q t) -> p q t", t=10)
            utok = ("ub", c % 2)
            if c == 0:
                sc0 = xn_spec(s, ("c", l, 0), 0)
                sh0 = xn_spec(s, ("h", l, 0), 0)
                sb0 = xn_spec(s, ("b", l, 0), 0)
                em0 = mm_multi([sc0, sh0, sb0])
                em0[0]()
                em0[1]()
                gc, gh = sc0["g"], sh0["g"]
            else:
                gc = G()
                mm_group(gc, range(KC), lambda k: w_lhsT(s, ("c", l, cp), k, co), xn_rhs, lambda k: [("xn", k)],
                         w_tokens(s, ("c", l, cp)))
                gh = G()
                mm_group(gh, range(KC), lambda k: w_lhsT(s, ("h", l, cp), k, co), xn_rhs, lambda k: [("xn", k)],
                         w_tokens(s, ("h", l, cp)))
            ti = next_T()
            P.add("act", both(lambda e, ti=ti, gc=gc: e.activation(out=W(T[ti][:]), in_=pg(gc), func=AF.Identity, bias=ZB,
                                                              scale=1.0)),
                  reads=pt(gc), writes=[("T", ti)])

            def halo(e, c=c, cb=cb, cbs=cbs):
                e.tensor_copy(cb[:, 0:2], haloA[:, j, c, :])
                return e.tensor_copy(cbs[:, :, 0:2], stA_in[:, c, :].rearrange("p (q t) -> p q t", t=2))
            P.add("pool", halo, reads=["haloA", "stA_in"], writes=[utok])

            def chmul(e, ti=ti, gh=gh, cb=cb, cbs=cbs):
                e.tensor_tensor(out=cb[:, 2:1026], in0=T[ti][:, 0:NPR], in1=pg(gh, 0, NPR), op=ALU.mult)
                return e.tensor_tensor(out=cbs[:, :, 2:10],
                                       in0=T[ti][:, NPR:NT].rearrange("p (q t) -> p q t", t=8),
                                       in1=pg(gh, NPR, NSM).rearrange("p (q t) -> p q t", t=8), op=ALU.mult)
            P.add("dve", chmul, reads=pt(gh) + [("T", ti), utok], writes=[utok])

            def save(e, c=c, cb=cb, cbs=cbs):
                e.tensor_copy(stcol[:, c, 0:2], cb[:, 1024:1026])
                e.tensor_copy(stcol[:, c, 2:18].rearrange("p (q t) -> p q t", t=2), cbs[:, :, 8:10])
                return e.tensor_copy(haloA[:, j, c, :], cb[:, 1024:1026])
            P.add("pool", save, reads=[utok], writes=tk_big(20, 21) + ["haloA"])
            ty = next_T()
            tys = T[ty][:, NPR:NT].rearrange("p (q t) -> p q t", t=8)

            def tap0(e, c=c, cb=cb, cbs=cbs, ty=ty, tys=tys):
                e.activation(out=T[ty][:, 0:NPR], in_=cb[:, 0:NPR], func=AF.Identity, bias=ZB,
                             scale=vec(V_ACONV + j * 3 + 0, c))
                return e.activation(out=tys, in_=cbs[:, :, 0:8], func=AF.Identity, bias=ZB,
                                    scale=vec(V_ACONV + j * 3 + 0, c))
            P.add("act", tap0, reads=[utok, "vecs"], writes=[("T", ty)])

            def tapk(kk):
                def fn(e, c=c, cb=cb, cbs=cbs, ty=ty, tys=tys):
                    e.scalar_tensor_tensor(out=T[ty][:, 0:NPR], in0=cb[:, kk:kk + NPR],
                                           scalar=vec(V_ACONV + j * 3 + kk, c), in1=T[ty][:, 0:NPR],
                                           op0=ALU.mult, op1=ALU.add)
                    return e.scalar_tensor_tensor(out=tys, in0=cbs[:, :, kk:kk + 8],
                                                  scalar=vec(V_ACONV + j * 3 + kk, c), in1=tys,
                                                  op0=ALU.mult, op1=ALU.add)
                return fn
            P.add("dve", tapk(1), reads=[utok, ("T", ty), "vecs"], writes=[("T", ty)])
            P.add("dve", tapk(2), reads=[utok, ("T", ty), "vecs"], writes=[("T", ty)])
            if c == 0:
                gb = sb0["g"]
                em0[2]()
            else:
                gb = G()
                mm_group(gb, range(KC), lambda k: w_lhsT(s, ("b", l, cp), k, co), xn_rhs, lambda k: [("xn", k)],
                         w_tokens(s, ("b", l, cp)))
            if c % 2 == 1:
                wfree(s, ("c", l, cp))
                wfree(s, ("h", l, cp))
                wfree(s, ("b", l, cp))
            P.add("dve", both(lambda e, c=c, gb=gb, ty=ty: e.tensor_tensor(out=W(act_slot(c)), in0=pg(gb), in1=W(T[ty][:]),
                                                                      op=ALU.mult)),
                  reads=pt(gb) + [("T", ty)], writes=tk_big(c))
        out_proj(s, l, lambda k: act_slot(k), lambda k: tk_big(k))
        dsts = [(2, 18, sas[j, s * NSQ:(s + 1) * NSQ, :, :].rearrange("q t d -> (q t) d"))]
        if s == n_st - 1:
            dsts.append((0, 2, sap[j, :, :]))
        state_out(s, 18, dsts)

    def mixer_B(s, l):
        j = l // 2
        lns = {"pending": None, "first": True}

        def ln_flush():
            if lns["pending"] is not None:
                qa, qb = lns["pending"]
                lns["pending"] = None
                g1 = G()
                stats_mm(g1, qa)
                acc_into(2, g1, lns["first"])
                g2 = G()
                stats_mm(g2, qb)
                acc_into(0, g2, lns["first"])
                lns["first"] = False

        def ln_feed(c):
            ln_flush()
            qa = next_sq()
            qb = next_sq()
            P.add("act", lambda e, c=c, qa=qa: e.activation(out=sq[qa][:], in_=y_slot(c), func=AF.Identity, bias=ZB,
                                                            scale=1.0),
                  reads=tk_big(2 * c, 2 * c + 1), writes=[("sq", qa)])
            P.add("act", lambda e, c=c, qb=qb: e.activation(out=sq[qb][:], in_=y_slot(c), func=AF.Square, bias=ZB,
                                                            scale=1.0),
                  reads=tk_big(2 * c, 2 * c + 1), writes=[("sq", qb)])
            lns["pending"] = (qa, qb)
        rmsnorm(V_GMIX + l)

        def stage1(c):
            cp = c // 2
            co = (c % 2) * 128
            cb = ub[c % 2]
            cbs = cb[:, 1054:1358].rearrange("p (q t) -> p q t", t=38)
            utok = ("ub", c % 2)
            ui = c % 2
            wv = lambda k, c=c: vec(V_BCONV + j * 31 + k, c)

            def mkdiag(e, ui=ui, c=c):
                b0 = V_BCONV + j * 31
                nd = N_DVE_TAPS
                return e.tensor_tensor(out=diag[ui][:, nd:31, :],
                                       in0=identb[:].unsqueeze(1).broadcast_to([128, 31 - nd, 128]),
                                       in1=vecs[:, b0 + nd:b0 + 31, c:c + 1].broadcast_to([128, 31 - nd, 128]),
                                       op=ALU.mult)
            P.add("dve", mkdiag, reads=["vecs", "identb"], writes=[("diag", ui)])
            ga = G()
            mm_group(ga, range(KC), lambda k: w_lhsT(s, ("a", l, cp), k, co), xn_rhs, lambda k: [("xn", k)],
                     w_tokens(s, ("a", l, cp)))
            gg = G()
            mm_group(gg, range(KC), lambda k: w_lhsT(s, ("g", l, cp), k, co), xn_rhs, lambda k: [("xn", k)],
                     w_tokens(s, ("g", l, cp)))
            if c % 2 == 1:
                wfree(s, ("a", l, cp))
                wfree(s, ("g", l, cp))
            t1 = next_T()
            t2 = next_T()
            P.add("act", both(lambda e, c=c, ga=ga, t2=t2: e.activation(out=W(T[t2][:]), in_=pg(ga), func=AF.Identity,
                                                                   bias=vec(V_HALF + j * 2 + 0, c), scale=0.5)),
                  reads=pt(ga) + ["vhalf"], writes=[("T", t2)])
            P.add("act", both(lambda e, c=c, gg=gg, t1=t1: e.activation(out=W(T[t1][:]), in_=pg(gg), func=AF.Tanh,
                                                                   bias=vec(V_HALF + j * 2 + 1, c), scale=0.5)),
                  reads=pt(gg) + ["vhalf"], writes=[("T", t1)])

            def halo(e, c=c, cb=cb, cbs=cbs):
                e.tensor_copy(cb[:, 0:30], haloB[:, j, c, :])
                return e.tensor_copy(cbs[:, :, 0:30], stB_view[:, c, :].rearrange("p (q t) -> p q t", t=30))
            P.add("pool", halo, reads=["haloB"] + tk_big(16, 17, 18, 19), writes=[utok])

            def glu(e, cb=cb, cbs=cbs, t1=t1, t2=t2):
                e.scalar_tensor_tensor(out=cb[:, 30:1054], in0=T[t1][:, 0:NPR], scalar=1.0, in1=T[t2][:, 0:NPR],
                                       op0=ALU.add, op1=ALU.mult)
                return e.scalar_tensor_tensor(out=cbs[:, :, 30:38],
                                              in0=T[t1][:, NPR:NT].rearrange("p (q t) -> p q t", t=8), scalar=1.0,
                                              in1=T[t2][:, NPR:NT].rearrange("p (q t) -> p q t", t=8),
                                              op0=ALU.add, op1=ALU.mult)
            P.add("dve", glu, reads=[("T", t1), ("T", t2), utok], writes=[utok])

            def save(e, c=c, cb=cb, cbs=cbs):
                e.tensor_copy(stcol[:, c, 0:30], cb[:, 1024:1054])
                e.tensor_copy(stcol[:, c, 30:94].rearrange("p (q t) -> p q t", t=8), cbs[:, :, 30:38])
                return e.tensor_copy(haloB[:, j, c, :], cb[:, 1024:1054])
            P.add("pool", save, reads=[utok], writes=tk_big(20, 21) + ["haloB"])
            P.add("act", lambda e, ui=ui, cb=cb: e.activation(out=ub16[ui][:], in_=cb[:, 0:1358], func=AF.Identity,
                                                              bias=ZB, scale=1.0),
                  reads=[utok], writes=[("ub16", ui)])

        def stage2(c):
            if c < KC - 1:
                ln_flush()
            cb = ub[c % 2]
            utok = ("ub", c % 2)
            ui = c % 2
            gy = G()
            pieces = [(0, 512, "p", True), (512, 512, "p", True), (NPR, NSM, "s", True)]
            ND = N_DVE_TAPS
            cbs = cb[:, 1054:1358].rearrange("p (q t) -> p q t", t=38)
            yj = y_slot(c)
            yjs = yj[:, NPR:NT].rearrange("p (q t) -> p q t", t=8)
            wv = lambda k, c=c: vec(V_BCONV + j * 31 + k, c)

            for k in range(ND):
                def tapfn(e, k=k, c=c, cb=cb, cbs=cbs, yj=yj, yjs=yjs, wv=wv):
                    if k == 0:
                        e.tensor_scalar(yj[:, 0:NPR], cb[:, 0:NPR], wv(0), vec(V_BDW + j, c), ALU.mult, ALU.add)
                        return e.tensor_scalar(yjs, cbs[:, :, 0:8], wv(0), vec(V_BDW + j, c), ALU.mult, ALU.add)
                    e.scalar_tensor_tensor(out=yj[:, 0:NPR], in0=cb[:, k:k + NPR], scalar=wv(k),
                                           in1=yj[:, 0:NPR], op0=ALU.mult, op1=ALU.add)
                    return e.scalar_tensor_tensor(out=yjs, in0=cbs[:, :, k:k + 8], scalar=wv(k), in1=yjs,
                                                  op0=ALU.mult, op1=ALU.add)
                P.add("dve", tapfn, reads=[utok, "vecs"] + (tk_big(2 * c, 2 * c + 1) if k > 0 else []),
                      writes=tk_big(2 * c, 2 * c + 1))

            def conv_p(e, ui=ui, gy=gy):
                ins = None
                for k in range(ND, 31):
                    for (pc0, pn) in MMP[0:2]:
                        ins = e.matmul(pg(gy, pc0, pn), diag[ui][:, k, :], ub16[ui][:, k + pc0:k + pc0 + pn],
                                       start=(k == ND), stop=(k == 30))
                return ins
            P.add("pe", conv_p, reads=[("ub16", ui), ("diag", ui)], writes=[("psp", gy[0])])

            def conv_s(e, ui=ui, gy=gy):
                ins = None
                u16s = ub16[ui][:, 1054:1358].rearrange("p (q t) -> p q t", t=38)
                for k in range(ND, 31):
                    ins = e.matmul(pg(gy, NPR, NSM), diag[ui][:, k, :], u16s[:, :, k:k + 8], start=(k == ND),
                                   stop=(k == 30))
                return ins
            P.add("pe", conv_s, reads=[("ub16", ui), ("diag", ui)], writes=[("pss", gy[1])])
            P.add("dve", both(lambda e, c=c, gy=gy: e.tensor_tensor(out=W(y_slot(c)), in0=pg(gy), in1=W(y_slot(c)),
                                                                    op=ALU.add)),
                  reads=pt(gy) + tk_big(2 * c, 2 * c + 1), writes=tk_big(2 * c, 2 * c + 1))
            ln_feed(c)

        stage1(0)
        for c in range(1, KC):
            stage1(c)
            if c == KC - 1:
                preswitch()
            stage2(c - 1)
        stage2(KC - 1)
        ln_flush()
        P.add("act", lambda e: e.activation(out=S[1][:], in_=S[2][:], func=AF.Square, bias=ZB, scale=1.0),
              reads=[("S", 2), "cst"], writes=[("S", 1)])
        P.add("dve", lambda e: e.tensor_tensor(out=S[0][:], in0=S[0][:], in1=S[1][:], op=ALU.subtract),
              reads=[("S", 0), ("S", 1)], writes=[("S", 0)])
        P.add("act", lambda e: e.activation(out=S[1][:], in_=S[0][:], func=AF.Ln, bias=cst[:, 2:3], scale=1.0),
              reads=[("S", 0), "cst"], writes=[("S", 1)])
        P.add("act", lambda e: e.activation(out=S[1][:], in_=S[1][:], func=AF.Exp, bias=ZB, scale=-0.5),
              reads=[("S", 1), "cst"], writes=[("S", 1)])
        postswitch()

        def ln_sub(c):
            yj = y_slot(c)
            P.add("dve", lambda e, yj=yj: e.tensor_tensor(out=yj, in0=yj, in1=S[2][:], op=ALU.subtract),
                  reads=[("S", 2)] + tk_big(2 * c, 2 * c + 1), writes=tk_big(2 * c, 2 * c + 1))
        ln_sub(0)
        ln_sub(1)
        for c in range(KC):
            yj = y_slot(c)
            P.add("dve", lambda e, yj=yj: e.tensor_tensor(out=yj, in0=yj, in1=S[1][:], op=ALU.mult),
                  reads=[("S", 1)] + tk_big(2 * c, 2 * c + 1), writes=tk_big(2 * c, 2 * c + 1))
            if c + 2 < KC:
                ln_sub(c + 2)
            P.add("act", lambda e, c=c, yj=yj: e.activation(out=xn[:, c, :], in_=yj, func=AF.Silu,
                                                            bias=vec(V_LNB + j, c), scale=vec(V_LNG + j, c)),
                  reads=tk_big(2 * c, 2 * c + 1) + ["vecs"], writes=[("xn", c)])
        out_proj(s, l, xn_rhs, lambda k: [("xn", k)], bias_idx=V_BPW2 + j)
        dsts = [(30 + 8 * q, 38 + 8 * q, sbs[j, s * NSQ + q, 22:30, :]) for q in range(NSQ)]
        if s == n_st - 1:
            dsts.append((0, 30, sbp[j, :, :]))
        state_out(s, 94, dsts)

    def ffn(s, l):
        rmsnorm(V_GFFN + l)

        def evac(f, gg, gu):
            ti = next_T()
            P.add("act", both(lambda e, gg=gg, ti=ti: e.activation(out=W(T[ti][:]), in_=pg(gg), func=AF.Silu, bias=ZB,
                                                              scale=1.0)),
                  reads=pt(gg), writes=[("T", ti)])
            P.add("dve", both(lambda e, f=f, gu=gu, ti=ti: e.tensor_tensor(out=W(act_slot(f)), in0=pg(gu), in1=W(T[ti][:]),
                                                                      op=ALU.mult)),
                  reads=pt(gu) + [("T", ti)], writes=tk_big(f))

        sg0 = xn_spec(s, ("fg", l, 0), 0)
        su0 = xn_spec(s, ("fu", l, 0), 0)
        sg1 = xn_spec(s, ("fg", l, 0), 128)
        em = mm_multi([sg0, su0, sg1])
        em[0]()
        em[1]()
        evac(0, sg0["g"], su0["g"])
        em[2]()
        gu1 = G()
        mm_group(gu1, range(KC), lambda k: w_lhsT(s, ("fu", l, 0), k, 128), xn_rhs, lambda k: [("xn", k)],
                 w_tokens(s, ("fu", l, 0)))
        wfree(s, ("fg", l, 0))
        wfree(s, ("fu", l, 0))
        evac(1, sg1["g"], gu1)
        load_p(l, s)
        for f in range(2, FC):
            fp = f // 2
            co = (f % 2) * 128
            gg = G()
            mm_group(gg, range(KC), lambda k: w_lhsT(s, ("fg", l, fp), k, co), xn_rhs, lambda k: [("xn", k)],
                     w_tokens(s, ("fg", l, fp)))
            gu = G()
            mm_group(gu, range(KC), lambda k: w_lhsT(s, ("fu", l, fp), k, co), xn_rhs, lambda k: [("xn", k)],
                     w_tokens(s, ("fu", l, fp)))
            if f % 2 == 1:
                wfree(s, ("fg", l, fp))
                wfree(s, ("fu", l, fp))
            evac(f, gg, gu)
        nstate["pm"] = nstate["on"]
        for n in range(KC):
            key = ("fd", l, n)
            g = G()
            mm_group(g, range(FC), lambda k, key=key: w_lhsT(s, key, k, 0), lambda k: act_slot(k),
                     lambda k: tk_big(k), w_tokens(s, key))
            wfree(s, key)
            resid_add(n, g)

    def ple(s, l):
        rmsnorm(V_GPLE + l)
        nstate["ps"] = KC - 1

        def evac(n, gg, gp):
            t1 = next_T()
            t2 = next_T()
            P.add("act", both(lambda e, gg=gg, t1=t1: e.activation(out=W(T[t1][:]), in_=pg(gg), func=AF.Tanh, bias=ZB,
                                                              scale=0.5)),
                  reads=pt(gg), writes=[("T", t1)])
            P.add("act", both(lambda e, gp=gp, t2=t2: e.activation(out=W(T[t2][:]), in_=pg(gp), func=AF.Identity, bias=ZB,
                                                              scale=0.5)),
                  reads=pt(gp), writes=[("T", t2)])
            P.add("dve", lambda e, t1=t1, t2=t2: e.scalar_tensor_tensor(
                out=T[t2][:], in0=T[t1][:], scalar=1.0, in1=T[t2][:], op0=ALU.add, op1=ALU.mult),
                reads=[("T", t1), ("T", t2)], writes=[("T", t2)])
            P.add("dve", lambda e, n=n, t2=t2: e.tensor_tensor(out=h[:, n, :], in0=h[:, n, :], in1=T[t2][:],
                                                               op=ALU.add),
                reads=[("T", t2), ("h", n)], writes=[("h", n)])
            if n > 0:
                norm_feed(n - 1)
            if n == KC - 1:
                norm_feed(n)

        def proj_spec(n):
            return dict(g=G(), klist=range(2), lhsT_fn=lambda k: w_lhsT(s, ("pp", l, 0), k, n * 128),
                        rhs_fn=lambda k: pT[:, k, :], reads_fn=lambda k: ["pT"], wtok=w_tokens(s, ("pp", l, 0)))

        sg0 = xn_spec(s, ("pg", l, 0), 0)
        sp0 = proj_spec(0)
        sg1 = xn_spec(s, ("pg", l, 0), 128)
        em = mm_multi([sp0, sg0, sg1])
        em[0]()
        em[1]()
        evac(0, sg0["g"], sp0["g"])
        em[2]()
        wfree(s, ("pg", l, 0))
        sp1 = proj_spec(1)
        mm_group(sp1["g"], sp1["klist"], sp1["lhsT_fn"], sp1["rhs_fn"], sp1["reads_fn"], sp1["wtok"])
        evac(1, sg1["g"], sp1["g"])
        if l + 1 < depth:
            load_state(s, l + 1)
        for n in range(2, KC):
            cp = n // 2
            co = (n % 2) * 128
            gg = G()
            mm_group(gg, range(KC), lambda k: w_lhsT(s, ("pg", l, cp), k, co), xn_rhs, lambda k: [("xn", k)],
                     w_tokens(s, ("pg", l, cp)))
            if n % 2 == 1:
                wfree(s, ("pg", l, cp))
            gp = G()
            mm_group(gp, range(2), lambda k: w_lhsT(s, ("pp", l, 0), k, n * 128), lambda k: pT[:, k, :],
                     lambda k: ["pT"], w_tokens(s, ("pp", l, 0)))
            if n == KC - 1:
                wfree(s, ("pp", l, 0))
            evac(n, gg, gp)
        nstate["ps"] = 4

    def final(s):
        P.add("sp", lambda e: e.dma_start(out=gfin, in_=gfin_in[:, :]),
              writes=tk_big(0, 1), dma="c1")
        tiles = [(yp[s * NPR + i * 128: s * NPR + (i + 1) * 128, :], 128, i * 128) for i in range(8)]
        tiles.append((ys[s * NSM:(s + 1) * NSM, :], NSM, NPR))
        P.add("dve", lambda e: e.memset(ssq[:], 0.0), writes=[("ssq", c_) for c_ in range(16)])
        for ti_, (dst, R, col0) in enumerate(tiles):
            g = G(False)

            def fn(e, g=g, R=R, col0=col0):
                ins = None
                for c in range(KC):
                    ins = e.transpose(pg(g, c * 128, 128)[0:R, :], h[:, c, col0:col0 + R], ident[:, :])
                return ins
            P.add("pe", fn, reads=[("h", c) for c in range(KC)] + ["ident"], writes=pt(g))
            si = ti_ % 4
            col = ti_
            fst = xn32[:, si * D:(si + 1) * D]
            ftok = [("fstg", si)]
            xtoks = [("xn", c) for c in range(KC)]


            def sqsum(e, g=g, R=R, si=si, col=col):
                return e.activation(out=T[0][0:R, 0:1024], in_=pg(g, 0, 1024)[0:R, :], func=AF.Square,
                                    bias=cst[0:R, 0:1], scale=1.0, accum_out=ssq[0:R, col:col + 1])
            P.add("act", sqsum, reads=pt(g) + [("ssq", col), "cst"], writes=[("T", 0), ("ssq", col)])
            P.add("act", lambda e, R=R, col=col: e.activation(out=ssq[0:R, col:col + 1], in_=ssq[0:R, col:col + 1],
                                                              func=AF.Ln, bias=cst[0:R, 1:2], scale=1.0 / D),
                  reads=[("ssq", col), "cst"], writes=[("ssq", col)])
            P.add("act", lambda e, R=R, col=col: e.activation(out=ssq[0:R, col:col + 1], in_=ssq[0:R, col:col + 1],
                                                              func=AF.Exp, bias=cst[0:R, 0:1], scale=-0.5),
                  reads=[("ssq", col), "cst"], writes=[("ssq", col)])
            P.add("dve", lambda e, g=g, R=R, fst=fst, col=col: e.scalar_tensor_tensor(
                out=fst[0:R, :], in0=pg(g, 0, 1024)[0:R, :], scalar=ssq[0:R, col:col + 1], in1=gfin[0:R, :],
                op0=ALU.mult, op1=ALU.mult),
                reads=pt(g) + [("ssq", col)] + tk_big(0, 1), writes=ftok + (xtoks if ti_ == 0 else []))
            P.add("sp", lambda e, dst=dst, R=R, fst=fst: e.dma_start(out=dst, in_=fst[0:R, :]),
                  reads=ftok + xtoks, dma="fo%d" % si)

    prefetch_x(0)
    for s in range(n_st):
        nstate["on"] = True
        load_x(s)
        for c in range(KC):
            norm_feed(c)
        load_state(s, 0)
        for l in range(depth):
            if l % 2 == 0:
                mixer_A(s, l)
            else:
                mixer_B(s, l)
            ffn(s, l)
            nstate["on"] = (l < depth - 1)
            if l == depth - 1 and s + 1 < n_st:
                prefetch_x(s + 1)
            ple(s, l)
        final(s)
    assert ring["next"] == NP_, (ring["next"], NP_)
    P.emit(stack)
    stack.close()
    return nc


def make_in_maps(inputs):
    wall = pack_weights(inputs)
    vecs = pack_vecs(inputs)
    gfin = np.ascontiguousarray(np.broadcast_to(np.asarray(inputs["g_final"], np.float32).reshape(1, D), (128, D)))
    maps = []
    for c in range(NCORE):
        sl = slice(16 * c, 16 * (c + 1))
        maps.append({
            "xp": np.ascontiguousarray(inputs["x_prompt"][c]),
            "xs": np.ascontiguousarray(np.asarray(inputs["x_sample"][sl]).reshape(128, D)),
            "pp": np.ascontiguousarray(inputs["p_prompt"][:, c]),
            "ps": np.ascontiguousarray(np.asarray(inputs["p_sample"][:, sl]).reshape(DEPTH, 128, PLE)),
            "sa": np.ascontiguousarray(inputs["state_conv_a"][:, sl]),
            "sb": np.ascontiguousarray(inputs["state_conv_b"][:, sl]),
            "vecs_in": vecs,
            "gfin_in": gfin,
            "wall": wall,
        })
    return maps


def gather(results):
    yp = np.stack([r["yp"] for r in results])
    ys = np.concatenate([r["ys"].reshape(16, 8, D) for r in results], axis=0)
    sap = np.stack([r["sap"] for r in results], axis=1)
    sas = np.concatenate([r["sas"] for r in results], axis=1)
    sbp = np.stack([r["sbp"] for r in results], axis=1)
    sbs = np.concatenate([r["sbs"] for r in results], axis=1)
    return tuple(np.ascontiguousarray(a, dtype=np.float32) for a in (yp, ys, sap, sas, sbp, sbs))


def kernel(**inputs):
    inputs = {k: np.asarray(v) for k, v in inputs.items()}
    nc = build_program()
    res = run_bass_kernel_spmd(nc, make_in_maps(inputs), core_ids=list(range(NCORE)))
    return gather(res.results)
```

```python
import contextlib
import numpy as np
import concourse.bass as bass
import concourse.mybir as mybir
from concourse.bass_utils import run_bass_kernel_spmd

F32 = mybir.dt.float32
BF16 = mybir.dt.bfloat16
ALU = mybir.AluOpType
AF = mybir.ActivationFunctionType

D = 1024
KC = 8
DFF = 2816
FC = 22
PLE = 256
DEPTH = 4
NCORE = 8
NT = 1088
NPR = 1024
NSQ = 8
NSM = 64
NST = 2
RMS_EPS = 1e-6
LN_EPS = 1e-5
N_DVE_TAPS = 6

V_GMIX, V_GFFN, V_GPLE = 0, 4, 8
V_ACONV = 12
V_BPW1 = 18
V_BDW = 22
V_LNG = 24
V_LNB = 26
V_BPW2 = 28
V_BCONV = 30
V_HALF = 92
NV = 96

RING_ELEMS = 12288
RBLK = 256
N_WSEM = 12


def weight_plan():
    plan = []
    for l in range(DEPTH):
        j = l // 2
        if l % 2 == 0:
            for cp in range(4):
                plan.append(("a_w_in", j, ("c", l, cp), 0, KC, 1024 + 256 * cp, 256))
                plan.append(("a_w_in", j, ("h", l, cp), 0, KC, 2048 + 256 * cp, 256))
                plan.append(("a_w_in", j, ("b", l, cp), 0, KC, 256 * cp, 256))
            for cp in range(4):
                plan.append(("a_w_out", j, ("o", l, cp), 0, KC, 256 * cp, 256))
        else:
            for cp in range(4):
                plan.append(("b_w_pw1", j, ("a", l, cp), 0, KC, 256 * cp, 256))
                plan.append(("b_w_pw1", j, ("g", l, cp), 0, KC, 1024 + 256 * cp, 256))
            for cp in range(4):
                plan.append(("b_w_pw2", j, ("o", l, cp), 0, KC, 256 * cp, 256))
        for fp in range(FC // 2):
            plan.append(("ffn_w_gu", l, ("fg", l, fp), 0, KC, 256 * fp, 256))
            plan.append(("ffn_w_gu", l, ("fu", l, fp), 0, KC, DFF + 256 * fp, 256))
        for n in range(KC):
            plan.append(("ffn_w_down", l, ("fd", l, n), 0, FC, 128 * n, 128))
        plan.append(("ple_w_proj", l, ("pp", l, 0), 0, 2, 0, 1024))
        for cp in range(4):
            plan.append(("ple_w_gate", l, ("pg", l, cp), 0, KC, 256 * cp, 256))
    return plan


def pack_weights(inputs):
    plan = weight_plan()
    total = sum(p[4] * p[6] for p in plan)
    wall = np.empty((128, total), dtype=np.float32)
    off = 0
    for (name, idx, key, k0, kn, n0, nn) in plan:
        w = np.asarray(inputs[name][idx], dtype=np.float32)
        blk = w[k0 * 128:(k0 + kn) * 128, n0:n0 + nn].reshape(kn, 128, nn).transpose(1, 0, 2)
        wall[:, off:off + kn * nn] = blk.reshape(128, kn * nn)
        off += kn * nn
    return wall


def pack_vecs(inputs):
    rows = [None] * NV
    for i in range(4):
        rows[V_GMIX + i] = inputs["g_mix"][i]
        rows[V_GFFN + i] = inputs["g_ffn"][i]
        rows[V_GPLE + i] = inputs["g_ple"][i]
    for j in range(2):
        for k in range(3):
            rows[V_ACONV + j * 3 + k] = inputs["a_conv"][j][k]
        rows[V_BPW1 + j * 2 + 0] = inputs["b_b_pw1"][j][:D]
        rows[V_BPW1 + j * 2 + 1] = inputs["b_b_pw1"][j][D:]
        rows[V_BDW + j] = inputs["b_b_conv"][j]
        rows[V_LNG + j] = inputs["b_ln_g"][j]
        rows[V_LNB + j] = inputs["b_ln_b"][j]
        rows[V_BPW2 + j] = inputs["b_b_pw2"][j]
        for k in range(31):
            rows[V_BCONV + j * 31 + k] = inputs["b_conv"][j][k]
    for i in range(NV):
        if rows[i] is None:
            rows[i] = np.zeros(D, np.float32)
    v = np.stack([np.asarray(r, np.float32) for r in rows])
    return np.ascontiguousarray(v.reshape(NV, KC, 128).transpose(2, 0, 1))


class Op:
    __slots__ = ("eng", "fn", "dma", "dval", "deps_c", "deps_d", "signal", "sig")

    def __init__(self, eng, fn, dma):
        self.eng = eng
        self.fn = fn
        self.dma = dma
        self.dval = 0
        self.deps_c = {}
        self.deps_d = {}
        self.signal = False
        self.sig = 0


class Prog:
    ENGS = ("pe", "act", "dve", "pool", "sp")

    def __init__(self, nc):
        self.nc = nc
        self.ops = []
        self.lastw = {}
        self.readers = {}
        self.dcount = {}

    AUTO_READS = {"act": ("cst", "vecs", "vhalf"), "dve": ("vecs", "vhalf", "identb"), "pool": ("vecs",),
                  "pe": ("ident", "ones")}

    def add(self, eng, fn, reads=(), writes=(), dma=None):
        op = Op(eng, fn, dma)
        idx = len(self.ops)
        if dma is None:
            wset = set(writes)
            reads = list(reads) + [t for t in self.AUTO_READS.get(eng, ()) if t not in wset]

        def dep(i, raw):
            o = self.ops[i]
            if o.dma is not None:
                if o.dval > op.deps_d.get(o.dma, 0):
                    op.deps_d[o.dma] = o.dval
                return
            if o.eng == eng:
                if eng == "pe" or not raw:
                    return
            if i > op.deps_c.get(o.eng, -1):
                op.deps_c[o.eng] = i

        for t in reads:
            lw = self.lastw.get(t)
            if lw is not None:
                dep(lw, True)
        for t in writes:
            lw = self.lastw.get(t)
            if lw is not None:
                dep(lw, False)
            for r in self.readers.get(t, ()):
                dep(r, False)
        for t in reads:
            self.readers.setdefault(t, []).append(idx)
        for t in writes:
            self.lastw[t] = idx
            self.readers[t] = []
        if dma is not None:
            self.dcount[dma] = self.dcount.get(dma, 0) + 16
            op.dval = self.dcount[dma]
        self.ops.append(op)
        return idx

    def emit(self, stack):
        nc = self.nc
        for op in self.ops:
            for e, i in op.deps_c.items():
                self.ops[i].signal = True
        eng_h = {"pe": nc.tensor, "act": nc.scalar, "dve": nc.vector, "pool": nc.gpsimd, "sp": nc.sync}
        esem = {e: stack.enter_context(nc.semaphore("s_" + e)) for e in self.ENGS}
        dsem = {k: stack.enter_context(nc.semaphore("d_" + str(k))) for k in self.dcount}
        count = {e: 0 for e in self.ENGS}
        waited = {e: {} for e in self.ENGS}
        for op in self.ops:
            h = eng_h[op.eng]
            w = waited[op.eng]
            for e, i in op.deps_c.items():
                v = self.ops[i].sig
                assert v > 0
                if w.get(("c", e), 0) < v:
                    h.wait_ge(esem[e], v)
                    w[("c", e)] = v
            for k, v in op.deps_d.items():
                if w.get(("d", k), 0) < v:
                    h.wait_ge(dsem[k], v)
                    w[("d", k)] = v
            ins = op.fn(h)
            if op.dma is not None:
                ins.then_inc(dsem[op.dma], 16)
            elif op.signal:
                count[op.eng] += 1
                op.sig = count[op.eng]
                ins.then_inc(esem[op.eng], 1)
        h = nc.sync
        for k, v in self.dcount.items():
            if waited["sp"].get(("d", k), 0) < v:
                h.wait_ge(dsem[k], v)
        for e in self.ENGS:
            if e != "sp" and count[e] > 0 and waited["sp"].get(("c", e), 0) < count[e]:
                h.wait_ge(esem[e], count[e])


def build_program(depth=DEPTH, n_st=NST):
    nc = bass.Bass("TRN2", target_bir_lowering=False)
    plan = weight_plan()
    wtotal = sum(p[4] * p[6] for p in plan)

    def din(name, shape):
        return nc.dram_tensor(name, list(shape), F32, kind="ExternalInput").ap()

    def dout(name, shape):
        return nc.dram_tensor(name, list(shape), F32, kind="ExternalOutput").ap()

    xp = din("xp", (2048, D))
    xs = din("xs", (128, D))
    pp = din("pp", (DEPTH, 2048, PLE))
    ps_in = din("ps", (DEPTH, 128, PLE))
    sa_in = din("sa", (2, 16, 2, D))
    sb_in = din("sb", (2, 16, 30, D))
    vecs_in = din("vecs_in", (128, NV, KC))
    gfin_in = din("gfin_in", (128, D))
    wall = din("wall", (128, wtotal))
    yp = dout("yp", (2048, D))
    ys = dout("ys", (128, D))
    sap = dout("sap", (2, 2, D))
    sas = dout("sas", (2, 16, 2, D))
    sbp = dout("sbp", (2, 30, D))
    sbs = dout("sbs", (2, 16, 30, D))

    stack = contextlib.ExitStack()

    def sb(name, shape, dt=F32):
        return stack.enter_context(nc.sbuf_tensor(name, list(shape), dt))

    h = sb("h", (128, KC, NT))
    xn = sb("xn", (128, KC, NT), BF16)
    big = sb("big", (128, FC * NT), BF16)
    wring = sb("wring", (128, RING_ELEMS), BF16)
    ub = [sb("ub%d" % i, (128, 1358)) for i in range(2)]
    ub16 = [sb("ub16_%d" % i, (128, 1358), BF16) for i in range(2)]
    diag = [sb("diag%d" % i, (128, 31, 128), BF16) for i in range(2)]
    identb = sb("identb", (128, 128), BF16)
    T = [sb("T%d" % i, (128, NT)) for i in range(3)]
    sq = [sb("sq%d" % i, (128, NT), BF16) for i in range(3)]
    S = [sb("S%d" % i, (128, NT)) for i in range(3)]
    pT = sb("pT", (128, 2, NT), BF16)
    stg = [sb("stg%d" % i, (128, D)) for i in range(2)]
    stgp = sb("stgp", (128, PLE))
    stA_in = sb("stA_in", (128, KC, 16))
    haloA = sb("haloA", (128, 2, KC, 2))
    haloB = sb("haloB", (128, 2, KC, 30))
    vecs = sb("vecs", (128, NV, KC))
    ident = sb("ident", (128, 128))
    onesD = sb("onesD", (128, 128), BF16)
    cst = sb("cst", (128, 4))
    ssq = sb("ssq", (128, 16))
    psum = stack.enter_context(nc.psum_tensor("psumt", [128, 4096], F32))

    big32 = big[:].bitcast(F32) if hasattr(big[:], "bitcast") else None
    assert big32 is not None
    stB_view = big32[:, 8 * NT: 8 * NT + KC * 240].rearrange("p (c r) -> p c r", r=240)
    stcol = big32[:, 10 * NT: 10 * NT + KC * 94].rearrange("p (c r) -> p c r", r=94)
    gfin = big32[:, 0:D]

    xn32 = xn[:].rearrange("p c t -> p (c t)").bitcast(F32)

    def act_slot(f):
        return big[:, f * NT:(f + 1) * NT]

    def y_slot(j):
        return big32[:, j * NT:(j + 1) * NT]

    def tk_big(*fs):
        return [("big", f) for f in fs]

    P = Prog(nc)
    ZB = cst[:, 0:1]

    gstate = {"g": 0, "gs": 0, "stg": 0, "T": 0, "sq": 0}

    held = {"p": None, "s": None}

    def G(sample=True):
        gi = gstate["g"]
        if gi == held["p"]:
            gi = (gi + 1) % 3
        gstate["g"] = (gi + 1) % 3
        si = None
        if sample:
            si = gstate["gs"]
            if si == held["s"]:
                si = (si + 1) % 2
            gstate["gs"] = (si + 1) % 2
        return (gi, si)

    def pt(g):
        return [("psp", g[0])] + ([("pss", g[1])] if g[1] is not None else [])

    win = {"c0": 0, "n": NT}

    def W(ap):
        return ap[:, win["c0"]:win["c0"] + win["n"]]

    def both(f):
        def fn(e):
            win.update(c0=0, n=NPR)
            f(e)
            win.update(c0=NPR, n=NSM)
            r = f(e)
            win.update(c0=0, n=NT)
            return r
        return fn

    MMP = [(0, 512), (512, 512), (1024, 64)]

    def next_stg():
        i = gstate["stg"]
        gstate["stg"] = (i + 1) % 2
        return i

    def next_T():
        i = gstate["T"]
        gstate["T"] = (i + 1) % 3
        return i

    def pg(g, c0=None, n=None):
        if c0 is None:
            c0, n = win["c0"], win["n"]
        if c0 + n <= NPR:
            b = g[0] * 1024 + c0
        else:
            assert c0 >= NPR and g[1] is not None, (g, c0, n)
            b = 3072 + g[1] * 512 + (c0 - NPR)
        return psum[:, b:b + n]

    ring = {"off": 0, "next": 0, "live": [], "resident": {}}
    wplan = [(s,) + p for s in range(n_st) for p in plan if p[2][1] < depth]
    NP_ = len(wplan)
    wsrc_off = {}
    o = 0
    for p in plan:
        wsrc_off[p[2]] = o
        o += p[4] * p[6]

    def ring_tokens(off, size):
        return [("wr", b) for b in range(off // RBLK, (off + size - 1) // RBLK + 1)]

    def prefetch():
        while ring["next"] < NP_:
            (s, name, idx, key, k0, kn, n0, nn) = wplan[ring["next"]]
            size = kn * nn
            live = ring["live"]
            off = ring["off"]
            if off + size > RING_ELEMS:
                off = 0
            ok = True
            for (lo, ls, _) in live:
                if lo < off + size and off < lo + ls:
                    ok = False
                    break
            if not ok:
                return
            q = ring["next"]
            ring["next"] += 1
            ring["off"] = off + size
            live.append((off, size, (s, key)))
            ring["resident"][(s, key)] = (off, kn, nn)
            so = wsrc_off[key]
            P.add("pool",
                  (lambda e, off=off, size=size, so=so:
                   e.dma_start(out=wring[:, off:off + size], in_=wall[:, so:so + size])),
                  reads=ring.pop("gate", []), writes=ring_tokens(off, size), dma="w%d" % (q % N_WSEM))

    def wpiece(s, key):
        off, kn, nn = ring["resident"][(s, key)]
        return off, kn, nn

    def wfree(s, key):
        ring["live"] = [x for x in ring["live"] if x[2] != (s, key)]
        del ring["resident"][(s, key)]
        prefetch()

    def w_lhsT(s, key, k, c0):
        off, kn, nn = wpiece(s, key)
        a = off + k * nn + c0
        return wring[:, a:a + 128]

    def w_tokens(s, key):
        off, kn, nn = wpiece(s, key)
        return ring_tokens(off, kn * nn)

    def mm_group(g, klist, lhsT_fn, rhs_fn, reads_fn, wtok):
        klist = list(klist)
        nk = len(klist)
        lts = [lhsT_fn(k) for k in klist]
        rs = [rhs_fn(k) for k in klist]
        allreads = []
        for i, k in enumerate(klist):
            rd = list(reads_fn(k))
            allreads += rd

            def fn(e, i=i):
                ins = None
                for (c0, n) in MMP[0:2]:
                    ins = e.matmul(pg(g, c0, n), lts[i], rs[i][:, c0:c0 + n], start=(i == 0), stop=(i == nk - 1))
                return ins
            P.add("pe", fn, reads=rd + list(wtok), writes=[("psp", g[0])])

        def fns(e):
            ins = None
            (c0, n) = MMP[2]
            for i in range(nk):
                ins = e.matmul(pg(g, c0, n), lts[i], rs[i][:, c0:c0 + n], start=(i == 0), stop=(i == nk - 1))
            return ins
        P.add("pe", fns, reads=allreads + list(wtok), writes=[("pss", g[1])])

    def mm_multi(specs):
        prep = []
        for sp in specs:
            kl = list(sp["klist"])
            prep.append(dict(g=sp["g"], nk=len(kl), lts=[sp["lhsT_fn"](k) for k in kl],
                             rs=[sp["rhs_fn"](k) for k in kl], rds=[list(sp["reads_fn"](k)) for k in kl],
                             wtok=list(sp["wtok"])))
        def emit_k(p, i):
            def fn(e, i=i, p=p):
                ins = None
                for (c0, n) in MMP[0:2]:
                    ins = e.matmul(pg(p["g"], c0, n), p["lts"][i], p["rs"][i][:, c0:c0 + n], start=(i == 0),
                                   stop=(i == p["nk"] - 1))
                return ins
            P.add("pe", fn, reads=p["rds"][i] + p["wtok"], writes=[("psp", p["g"][0])])

        TAIL = 2
        SKEW = 2
        for i in range(max(p["nk"] for p in prep) + SKEW):
            for gi, p in enumerate(prep):
                ii = i - (SKEW if gi >= 2 else 0)
                if 0 <= ii < p["nk"] - TAIL:
                    emit_k(p, ii)
        for p in prep:
            for i in range(max(p["nk"] - TAIL, 0), p["nk"]):
                emit_k(p, i)
        ems = []
        for p in prep:
            def em(p=p):
                def fns(e):
                    ins = None
                    (c0, n) = MMP[2]
                    for i in range(p["nk"]):
                        ins = e.matmul(pg(p["g"], c0, n), p["lts"][i], p["rs"][i][:, c0:c0 + n], start=(i == 0),
                                       stop=(i == p["nk"] - 1))
                    return ins
                P.add("pe", fns, reads=[t for r in p["rds"] for t in r] + p["wtok"], writes=[("pss", p["g"][1])])
            ems.append(em)
        return ems

    def xn_spec(s_, key, co):
        return dict(g=G(), klist=range(KC), lhsT_fn=lambda k: w_lhsT(s_, key, k, co), rhs_fn=xn_rhs,
                    reads_fn=lambda k: [("xn", k)], wtok=w_tokens(s_, key))

    def xn_rhs(k):
        return xn[:, k, :]

    def tr_in(g, src, R, nchunk, blk0=0):
        def fn(e):
            ins = None
            for c in range(nchunk):
                ins = e.transpose(pg(g, (blk0 + c) * 128, R), src[0:R, c * 128:(c + 1) * 128], ident[0:R, 0:R])
            return ins
        return fn

    P.add("sp", lambda e: e.dma_start(out=vecs[:], in_=vecs_in[:, :, :]), writes=["vecs"], dma="c0")

    P.add("pool", lambda e: e.memset(ident[:], 0.0), writes=["ident"])
    P.add("pool", lambda e: e.affine_select(out=ident[:], in_=ident[:], compare_op=ALU.not_equal, fill=1.0, base=0,
                                            pattern=[[-1, 128]], channel_multiplier=1),
          reads=["ident"], writes=["ident"])
    P.add("pool", lambda e: e.tensor_copy(identb[:], ident[:]), reads=["ident"], writes=["identb"])

    def setup_pool(e):
        e.memset(onesD[:], 1.0 / D)
        e.memset(cst[:, 0:1], 0.0)
        e.memset(cst[:, 1:2], RMS_EPS)
        e.memset(cst[:, 2:3], LN_EPS)
        e.memset(haloA[:], 0.0)
        return e.memset(haloB[:], 0.0)
    P.add("pool", setup_pool, writes=["ones", "cst", "haloA", "haloB"])
    P.add("dve", lambda e: e.tensor_scalar(vecs[:, V_HALF:V_HALF + 4, :], vecs[:, V_BPW1:V_BPW1 + 4, :], 0.5, None,
                                           ALU.mult),
          reads=["vecs"], writes=["vhalf"])

    def vec(i, c):
        return vecs[:, i, c:c + 1]

    XS0 = 2 * NT
    XTOK = tk_big(*range(4, 21))

    def prefetch_x(s):
        for hf in range(2):
            P.add("sp", lambda e, hf=hf: e.dma_start(
                out=big32[:, XS0 + hf * 4 * D:XS0 + (hf + 1) * 4 * D].rearrange("p (t d) -> p t d", d=D),
                in_=xp[s * NPR + hf * 512:s * NPR + (hf + 1) * 512, :].rearrange("(t p) d -> p t d", p=128)),
                writes=(XTOK if hf == 0 else []) + [("xst", 4 * hf + t_) for t_ in range(4)], dma="xl%d" % hf)
        P.add("sp", lambda e: e.dma_start(out=big32[0:NSM, XS0 + 8 * D:XS0 + 9 * D], in_=xs[s * NSM:(s + 1) * NSM, :]),
              writes=[("xst", 8)], dma="xl8")

    def load_x(s):
        tiles = [(128, i * 128) for i in range(8)] + [(NSM, NPR)]
        for i, (R, col0) in enumerate(tiles):
            src = big32[:, XS0 + i * D:XS0 + (i + 1) * D]
            g = G(False)
            P.add("pe", tr_in(g, src, R, KC), reads=XTOK + [("xst", i), "ident"], writes=pt(g))
            if i % 2 == 0:
                P.add("act", lambda e, g=g, R=R, col0=col0: e.activation(
                    out=h[:, :, col0:col0 + R],
                    in_=pg(g, 0, 1024).rearrange("p (c r) -> p c r", r=128)[:, :, 0:R],
                    func=AF.Identity, bias=ZB, scale=1.0),
                    reads=pt(g), writes=[("h", c) for c in range(KC)])
            else:
                P.add("dve", lambda e, g=g, R=R, col0=col0: e.tensor_copy(
                    h[:, :, col0:col0 + R], pg(g, 0, 1024).rearrange("p (c r) -> p c r", r=128)[:, :, 0:R]),
                    reads=pt(g), writes=[("h", c) for c in range(KC)])

    def load_p(l, s):
        groups = [(pp[l, s * NPR + gi * 512: s * NPR + (gi + 1) * 512, :], 4, 128, gi * 512) for gi in range(2)]
        groups.append((ps_in[l, s * NSM:(s + 1) * NSM, :], 1, NSM, NPR))
        staged = []
        for (src, nt, R, col0) in groups:
            if nt == 4:
                si = next_stg()
                buf, tok = stg[si], ("stg", si)
                P.add("sp", lambda e, src=src, buf=buf: e.dma_start(
                    out=buf[:, :].rearrange("p (t d) -> p t d", d=PLE),
                    in_=src.rearrange("(t p) d -> p t d", p=128)),
                    writes=[tok], dma="stg%d" % si)
            else:
                buf, tok = stgp, "stgp"
                P.add("sp", lambda e, src=src, buf=buf, R=R: e.dma_start(out=buf[0:R, 0:PLE], in_=src),
                      writes=[tok], dma="stgp")
            staged.append((buf, tok))
        for (src, nt, R, col0), (buf, tok) in zip(groups, staged):
            g = G(False)

            def fn(e, g=g, buf=buf, nt=nt, R=R):
                ins = None
                for t in range(nt):
                    for c in range(2):
                        ins = e.transpose(pg(g, (t * 2 + c) * 128, R),
                                          buf[0:R, t * PLE + c * 128: t * PLE + (c + 1) * 128],
                                          ident[0:R, 0:R])
                return ins
            P.add("pe", fn, reads=[tok, "ident"], writes=pt(g))

            def ev(e, g=g, nt=nt, R=R, col0=col0):
                ins = None
                for c in range(2):
                    src_v = pg(g, 0, nt * 256).rearrange("p (t c r) -> p t c r", c=2, r=128)[:, :, c, 0:R]
                    dst_v = pT[:, c, col0:col0 + nt * R].rearrange("p (t r) -> p t r", r=R)
                    ins = e.activation(out=dst_v, in_=src_v, func=AF.Identity, bias=ZB, scale=1.0)
                return ins
            P.add("act", ev, reads=pt(g), writes=["pT"])

    nstate = {"pending": None, "first": True, "on": True, "ps": 4, "pm": False, "hg": None}

    def postswitch():
        P.add("act", lambda e: e.activation(out=cst[:, 3:4], in_=cst[:, 0:1], func=AF.Tanh, bias=ZB, scale=1.0),
              reads=["cst"], writes=["cstdummy"])

    def preswitch():
        P.add("act", lambda e: e.activation(out=cst[:, 3:4], in_=cst[:, 1:2], func=AF.Ln, bias=ZB, scale=1.0),
              reads=["cst"], writes=["cstdummy"])

    def next_sq():
        qi = gstate["sq"]
        gstate["sq"] = (qi + 1) % len(sq)
        return qi

    def stats_mm(g, qi, start=True, stop=True):
        def fn(e):
            ins = None
            for (c0, n) in MMP:
                ins = e.matmul(pg(g, c0, n), onesD[:], sq[qi][:, c0:c0 + n], start=start, stop=stop)
            return ins
        P.add("pe", fn, reads=[("sq", qi), "ones"], writes=pt(g))

    def acc_into(ix, g, first):
        Sx = S[ix]
        if first:
            P.add("dve", both(lambda e: e.tensor_copy(W(Sx[:]), pg(g))), reads=pt(g), writes=[("S", ix)])
        else:
            P.add("dve", both(lambda e: e.tensor_tensor(out=W(Sx[:]), in0=pg(g), in1=W(Sx[:]), op=ALU.add)),
                  reads=pt(g) + [("S", ix)], writes=[("S", ix)])

    def norm_flush(final=False):
        if nstate["pending"] is not None:
            qi = nstate["pending"]
            nstate["pending"] = None
            if nstate["pm"]:
                if nstate["hg"] is None:
                    nstate["hg"] = G()
                    held["p"], held["s"] = nstate["hg"]
                stats_mm(nstate["hg"], qi, start=nstate["first"], stop=final)
            else:
                g = G()
                stats_mm(g, qi)
                acc_into(2, g, nstate["first"])
            nstate["first"] = False

    def norm_feed(c):
        if not nstate["on"]:
            return
        norm_flush()
        if c == nstate["ps"]:
            preswitch()
        qi = next_sq()
        P.add("act", lambda e, c=c, qi=qi: e.activation(out=sq[qi][:], in_=h[:, c, :], func=AF.Square,
                                                        bias=ZB, scale=1.0),
              reads=[("h", c)], writes=[("sq", qi)])
        nstate["pending"] = qi

    def rmsnorm(vidx):
        norm_flush(final=True)
        nstate["first"] = True
        if nstate["hg"] is not None:
            hg = nstate["hg"]
            P.add("act", both(lambda e, hg=hg: e.activation(out=W(S[0][:]), in_=pg(hg), func=AF.Ln, bias=cst[:, 1:2],
                                                            scale=1.0)),
                  reads=pt(hg) + ["cst"], writes=[("S", 0)])
            nstate["hg"] = None
            held["p"] = held["s"] = None
        else:
            P.add("act", lambda e: e.activation(out=S[0][:], in_=S[2][:], func=AF.Ln, bias=cst[:, 1:2], scale=1.0),
                  reads=[("S", 2), "cst"], writes=[("S", 0)])
        nstate["pm"] = False
        P.add("act", lambda e: e.activation(out=S[1][:], in_=S[0][:], func=AF.Exp, bias=ZB, scale=-0.5),
              reads=[("S", 0), "cst"], writes=[("S", 1)])
        postswitch()
        for c in range(KC):
            P.add("dve", lambda e, c=c: e.scalar_tensor_tensor(out=xn[:, c, :], in0=h[:, c, :], scalar=vec(vidx, c),
                                                               in1=S[1][:], op0=ALU.mult, op1=ALU.mult),
                  reads=[("h", c), ("S", 1), "vecs"], writes=[("xn", c)])

    def resid_add(n, g, bias_idx=None):
        if bias_idx is None:
            P.add("dve", both(lambda e, n=n, g=g: e.tensor_tensor(out=W(h[:, n, :]), in0=pg(g), in1=W(h[:, n, :]), op=ALU.add)),
                  reads=pt(g) + [("h", n)], writes=[("h", n)])
            norm_feed(n)
        else:
            P.add("dve", both(lambda e, n=n, g=g: e.scalar_tensor_tensor(out=W(h[:, n, :]), in0=pg(g), scalar=vec(bias_idx, n),
                                                                    in1=W(h[:, n, :]), op0=ALU.add, op1=ALU.add)),
                  reads=pt(g) + [("h", n), "vecs"], writes=[("h", n)])
            norm_feed(n)

    def out_proj(s, l, rhs_fn, rtok_fn, bias_idx=None):
        nstate["pm"] = nstate["on"]

        def spec(n):
            key = ("o", l, n // 2)
            return dict(g=G(), klist=range(KC), lhsT_fn=lambda k: w_lhsT(s, key, k, (n % 2) * 128), rhs_fn=rhs_fn,
                        reads_fn=rtok_fn, wtok=w_tokens(s, key))
        sp = [spec(0), spec(1), spec(2)]
        em = mm_multi(sp)
        em[0]()
        em[1]()
        wfree(s, ("o", l, 0))
        resid_add(0, sp[0]["g"], bias_idx)
        em[2]()
        resid_add(1, sp[1]["g"], bias_idx)
        resid_add(2, sp[2]["g"], bias_idx)
        for n in range(3, KC):
            key = ("o", l, n // 2)
            g = G()
            mm_group(g, range(KC), lambda k, key=key, n=n: w_lhsT(s, key, k, (n % 2) * 128), rhs_fn, rtok_fn,
                     w_tokens(s, key))
            if n % 2 == 1:
                wfree(s, key)
            resid_add(n, g, bias_idx)

    def state_out(s, ncols, dsts):
        g = G(False)

        def fn(e):
            ins = None
            for c in range(KC):
                ins = e.transpose(pg(g, c * 128, 128)[0:ncols, :], stcol[:, c, 0:ncols], ident[:, :])
            return ins
        P.add("pe", fn, reads=tk_big(20, 21) + ["ident"], writes=pt(g))
        si = next_stg()
        P.add("act", lambda e: e.activation(out=stg[si][0:ncols, :], in_=pg(g, 0, 1024)[0:ncols, :], func=AF.Identity,
                                            bias=cst[0:ncols, 0:1], scale=1.0),
              reads=pt(g) + ["cst"], writes=[("stg", si)])
        for (r0, r1, dst) in dsts:
            P.add("sp", lambda e, r0=r0, r1=r1, dst=dst: e.dma_start(out=dst, in_=stg[si][r0:r1, :]),
                  reads=[("stg", si)], dma="out%d" % si)

    def load_state(s, l):
        j = l // 2
        if l % 2 == 0:
            si = next_stg()
            P.add("sp", lambda e: e.dma_start(out=stg[si][0:16, :],
                                              in_=sa_in[j, s * NSQ:(s + 1) * NSQ, :, :].rearrange("q t d -> (q t) d")),
                  writes=[("stg", si)], dma="stg%d" % si)
            g0 = G(False)
            P.add("pe", tr_in(g0, stg[si], 16, KC), reads=[("stg", si), "ident"], writes=pt(g0))
            P.add("act", lambda e: e.activation(out=stA_in[:, :, :],
                                                in_=pg(g0, 0, 1024).rearrange("p (c r) -> p c r", r=128)[:, :, 0:16],
                                                func=AF.Identity, bias=ZB, scale=1.0),
                  reads=pt(g0), writes=["stA_in"])
        else:
            for (r0, R) in ((0, 128), (128, 112)):
                si = next_stg()
                P.add("sp", lambda e, si=si, r0=r0, R=R: e.dma_start(
                    out=stg[si][0:R, :],
                    in_=sb_in[j, s * NSQ:(s + 1) * NSQ, :, :].rearrange("q t d -> (q t) d")[r0:r0 + R, :]),
                    writes=[("stg", si)], dma="stg%d" % si)
                g0 = G(False)
                P.add("pe", tr_in(g0, stg[si], R, KC), reads=[("stg", si), "ident"], writes=pt(g0))
                P.add("act", lambda e, g0=g0, r0=r0, R=R: e.activation(
                    out=stB_view[:, :, r0:r0 + R],
                    in_=pg(g0, 0, 1024).rearrange("p (c r) -> p c r", r=128)[:, :, 0:R],
                    func=AF.Identity, bias=ZB, scale=1.0),
                    reads=pt(g0), writes=tk_big(16, 17, 18, 19))
            P.add("sp", lambda e: e.dma_start(out=sbs[j, s * NSQ:(s + 1) * NSQ, 0:22, :],
                                              in_=sb_in[j, s * NSQ:(s + 1) * NSQ, 8:30, :]), dma="dd")

    def mixer_A(s, l):
        j = l // 2
        rmsnorm(V_GMIX + l)
        for c in range(KC):
            cp = c // 2
            co = (c % 2) * 128
            cb = ub[c % 2]
            cbs = cb[:, 1026:1106].rearrange("p (q t) -> p q t", t=10)
            utok = ("ub", c % 2)
            if c == 0:
                sc0 = xn_spec(s, ("c", l, 0), 0)
                sh0 = xn_spec(s, ("h", l, 0), 0)
                sb0 = xn_spec(s, ("b", l, 0), 0)
                em0 = mm_multi([sc0, sh0, sb0])
                em0[0]()
                em0[1]()
                gc, gh = sc0["g"], sh0["g"]
            else:
                gc = G()
                mm_group(gc, range(KC), lambda k: w_lhsT(s, ("c", l, cp), k, co), xn_rhs, lambda k: [("xn", k)],
                         w_tokens(s, ("c", l, cp)))
                gh = G()
                mm_group(gh, range(KC), lambda k: w_lhsT(s, ("h", l, cp), k, co), xn_rhs, lambda k: [("xn", k)],
                         w_tokens(s, ("h", l, cp)))
            ti = next_T()
            P.add("act", both(lambda e, ti=ti, gc=gc: e.activation(out=W(T[ti][:]), in_=pg(gc), func=AF.Identity, bias=ZB,
                                                              scale=1.0)),
                  reads=pt(gc), writes=[("T", ti)])

            def halo(e, c=c, cb=cb, cbs=cbs):
                e.tensor_copy(cb[:, 0:2], haloA[:, j, c, :])
                return e.tensor_copy(cbs[:, :, 0:2], stA_in[:, c, :].rearrange("p (q t) -> p q t", t=2))
            P.add("pool", halo, reads=["haloA", "stA_in"], writes=[utok])

            def chmul(e, ti=ti, gh=gh, cb=cb, cbs=cbs):
                e.tensor_tensor(out=cb[:, 2:1026], in0=T[ti][:, 0:NPR], in1=pg(gh, 0, NPR), op=ALU.mult)
                return e.tensor_tensor(out=cbs[:, :, 2:10],
                                       in0=T[ti][:, NPR:NT].rearrange("p (q t) -> p q t", t=8),
                                       in1=pg(gh, NPR, NSM).rearrange("p (q t) -> p q t", t=8), op=ALU.mult)
            P.add("dve", chmul, reads=pt(gh) + [("T", ti), utok], writes=[utok])

            def save(e, c=c, cb=cb, cbs=cbs):
                e.tensor_copy(stcol[:, c, 0:2], cb[:, 1024:1026])
                e.tensor_copy(stcol[:, c, 2:18].rearrange("p (q t) -> p q t", t=2), cbs[:, :, 8:10])
                return e.tensor_copy(haloA[:, j, c, :], cb[:, 1024:1026])
            P.add("pool", save, reads=[utok], writes=tk_big(20, 21) + ["haloA"])
            ty = next_T()
            tys = T[ty][:, NPR:NT].rearrange("p (q t) -> p q t", t=8)

            def tap0(e, c=c, cb=cb, cbs=cbs, ty=ty, tys=tys):
                e.activation(out=T[ty][:, 0:NPR], in_=cb[:, 0:NPR], func=AF.Identity, bias=ZB,
                             scale=vec(V_ACONV + j * 3 + 0, c))
                return e.activation(out=tys, in_=cbs[:, :, 0:8], func=AF.Identity, bias=ZB,
                                    scale=vec(V_ACONV + j * 3 + 0, c))
            P.add("act", tap0, reads=[utok, "vecs"], writes=[("T", ty)])

            def tapk(kk):
                def fn(e, c=c, cb=cb, cbs=cbs, ty=ty, tys=tys):
                    e.scalar_tensor_tensor(out=T[ty][:, 0:NPR], in0=cb[:, kk:kk + NPR],
                                           scalar=vec(V_ACONV + j * 3 + kk, c), in1=T[ty][:, 0:NPR],
                                           op0=ALU.mult, op1=ALU.add)
                    return e.scalar_tensor_tensor(out=tys, in0=cbs[:, :, kk:kk + 8],
                                                  scalar=vec(V_ACONV + j * 3 + kk, c), in1=tys,
                                                  op0=ALU.mult, op1=ALU.add)
                return fn
            P.add("dve", tapk(1), reads=[utok, ("T", ty), "vecs"], writes=[("T", ty)])
            P.add("dve", tapk(2), reads=[utok, ("T", ty), "vecs"], writes=[("T", ty)])
            if c == 0:
                gb = sb0["g"]
                em0[2]()
            else:
                gb = G()
                mm_group(gb, range(KC), lambda k: w_lhsT(s, ("b", l, cp), k, co), xn_rhs, lambda k: [("xn", k)],
                         w_tokens(s, ("b", l, cp)))
            if c % 2 == 1:
                wfree(s, ("c", l, cp))
                wfree(s, ("h", l, cp))
                wfree(s, ("b", l, cp))
            P.add("dve", both(lambda e, c=c, gb=gb, ty=ty: e.tensor_tensor(out=W(act_slot(c)), in0=pg(gb), in1=W(T[ty][:]),
                                                                      op=ALU.mult)),
                  reads=pt(gb) + [("T", ty)], writes=tk_big(c))
        out_proj(s, l, lambda k: act_slot(k), lambda k: tk_big(k))
        dsts = [(2, 18, sas[j, s * NSQ:(s + 1) * NSQ, :, :].rearrange("q t d -> (q t) d"))]
        if s == n_st - 1:
            dsts.append((0, 2, sap[j, :, :]))
        state_out(s, 18, dsts)

    def mixer_B(s, l):
        j = l // 2
        lns = {"pending": None, "first": True}

        def ln_flush():
            if lns["pending"] is not None:
                qa, qb = lns["pending"]
                lns["pending"] = None
                g1 = G()
                stats_mm(g1, qa)
                acc_into(2, g1, lns["first"])
                g2 = G()
                stats_mm(g2, qb)
                acc_into(0, g2, lns["first"])
                lns["first"] = False

        def ln_feed(c):
            ln_flush()
            qa = next_sq()
            qb = next_sq()
            P.add("act", lambda e, c=c, qa=qa: e.activation(out=sq[qa][:], in_=y_slot(c), func=AF.Identity, bias=ZB,
                                                            scale=1.0),
                  reads=tk_big(2 * c, 2 * c + 1), writes=[("sq", qa)])
            P.add("act", lambda e, c=c, qb=qb: e.activation(out=sq[qb][:], in_=y_slot(c), func=AF.Square, bias=ZB,
                                                            scale=1.0),
                  reads=tk_big(2 * c, 2 * c + 1), writes=[("sq", qb)])
            lns["pending"] = (qa, qb)
        rmsnorm(V_GMIX + l)

        def stage1(c):
            cp = c // 2
            co = (c % 2) * 128
            cb = ub[c % 2]
            cbs = cb[:, 1054:1358].rearrange("p (q t) -> p q t", t=38)
            utok = ("ub", c % 2)
            ui = c % 2
            wv = lambda k, c=c: vec(V_BCONV + j * 31 + k, c)

            def mkdiag(e, ui=ui, c=c):
                b0 = V_BCONV + j * 31
                nd = N_DVE_TAPS
                return e.tensor_tensor(out=diag[ui][:, nd:31, :],
                                       in0=identb[:].unsqueeze(1).broadcast_to([128, 31 - nd, 128]),
                                       in1=vecs[:, b0 + nd:b0 + 31, c:c + 1].broadcast_to([128, 31 - nd, 128]),
                                       op=ALU.mult)
            P.add("dve", mkdiag, reads=["vecs", "identb"], writes=[("diag", ui)])
            ga = G()
            mm_group(ga, range(KC), lambda k: w_lhsT(s, ("a", l, cp), k, co), xn_rhs, lambda k: [("xn", k)],
                     w_tokens(s, ("a", l, cp)))
            gg = G()
            mm_group(gg, range(KC), lambda k: w_lhsT(s, ("g", l, cp), k, co), xn_rhs, lambda k: [("xn", k)],
                     w_tokens(s, ("g", l, cp)))
            if c % 2 == 1:
                wfree(s, ("a", l, cp))
                wfree(s, ("g", l, cp))
            t1 = next_T()
            t2 = next_T()
            P.add("act", both(lambda e, c=c, ga=ga, t2=t2: e.activation(out=W(T[t2][:]), in_=pg(ga), func=AF.Identity,
                                                                   bias=vec(V_HALF + j * 2 + 0, c), scale=0.5)),
                  reads=pt(ga) + ["vhalf"], writes=[("T", t2)])
            P.add("act", both(lambda e, c=c, gg=gg, t1=t1: e.activation(out=W(T[t1][:]), in_=pg(gg), func=AF.Tanh,
                                                                   bias=vec(V_HALF + j * 2 + 1, c), scale=0.5)),
                  reads=pt(gg) + ["vhalf"], writes=[("T", t1)])

            def halo(e, c=c, cb=cb, cbs=cbs):
                e.tensor_copy(cb[:, 0:30], haloB[:, j, c, :])
                return e.tensor_copy(cbs[:, :, 0:30], stB_view[:, c, :].rearrange("p (q t) -> p q t", t=30))
            P.add("pool", halo, reads=["haloB"] + tk_big(16, 17, 18, 19), writes=[utok])

            def glu(e, cb=cb, cbs=cbs, t1=t1, t2=t2):
                e.scalar_tensor_tensor(out=cb[:, 30:1054], in0=T[t1][:, 0:NPR], scalar=1.0, in1=T[t2][:, 0:NPR],
                                       op0=ALU.add, op1=ALU.mult)
                return e.scalar_tensor_tensor(out=cbs[:, :, 30:38],
                                              in0=T[t1][:, NPR:NT].rearrange("p (q t) -> p q t", t=8), scalar=1.0,
                                              in1=T[t2][:, NPR:NT].rearrange("p (q t) -> p q t", t=8),
                                              op0=ALU.add, op1=ALU.mult)
            P.add("dve", glu, reads=[("T", t1), ("T", t2), utok], writes=[utok])

            def save(e, c=c, cb=cb, cbs=cbs):
                e.tensor_copy(stcol[:, c, 0:30], cb[:, 1024:1054])
                e.tensor_copy(stcol[:, c, 30:94].rearrange("p (q t) -> p q t", t=8), cbs[:, :, 30:38])
                return e.tensor_copy(haloB[:, j, c, :], cb[:, 1024:1054])
            P.add("pool", save, reads=[utok], writes=tk_big(20, 21) + ["haloB"])
            P.add("act", lambda e, ui=ui, cb=cb: e.activation(out=ub16[ui][:], in_=cb[:, 0:1358], func=AF.Identity,
                                                              bias=ZB, scale=1.0),
                  reads=[utok], writes=[("ub16", ui)])

        def stage2(c):
            if c < KC - 1:
                ln_flush()
            cb = ub[c % 2]
            utok = ("ub", c % 2)
            ui = c % 2
            gy = G()
            pieces = [(0, 512, "p", True), (512, 512, "p", True), (NPR, NSM, "s", True)]
            ND = N_DVE_TAPS
            cbs = cb[:, 1054:1358].rearrange("p (q t) -> p q t", t=38)
            yj = y_slot(c)
            yjs = yj[:, NPR:NT].rearrange("p (q t) -> p q t", t=8)
            wv = lambda k, c=c: vec(V_BCONV + j * 31 + k, c)

            for k in range(ND):
                def tapfn(e, k=k, c=c, cb=cb, cbs=cbs, yj=yj, yjs=yjs, wv=wv):
                    if k == 0:
                        e.tensor_scalar(yj[:, 0:NPR], cb[:, 0:NPR], wv(0), vec(V_BDW + j, c), ALU.mult, ALU.add)
                        return e.tensor_scalar(yjs, cbs[:, :, 0:8], wv(0), vec(V_BDW + j, c), ALU.mult, ALU.add)
                    e.scalar_tensor_tensor(out=yj[:, 0:NPR], in0=cb[:, k:k + NPR], scalar=wv(k),
                                           in1=yj[:, 0:NPR], op0=ALU.mult, op1=ALU.add)
                    return e.scalar_tensor_tensor(out=yjs, in0=cbs[:, :, k:k + 8], scalar=wv(k), in1=yjs,
                                                  op0=ALU.mult, op1=ALU.add)
                P.add("dve", tapfn, reads=[utok, "vecs"] + (tk_big(2 * c, 2 * c + 1) if k > 0 else []),
                      writes=tk_big(2 * c, 2 * c + 1))

            def conv_p(e, ui=ui, gy=gy):
                ins = None
                for k in range(ND, 31):
                    for (pc0, pn) in MMP[0:2]:
                        ins = e.matmul(pg(gy, pc0, pn), diag[ui][:, k, :], ub16[ui][:, k + pc0:k + pc0 + pn],
                                       start=(k == ND), stop=(k == 30))
                return ins
            P.add("pe", conv_p, reads=[("ub16", ui), ("diag", ui)], writes=[("psp", gy[0])])

            def conv_s(e, ui=ui, gy=gy):
                ins = None
                u16s = ub16[ui][:, 1054:1358].rearrange("p (q t) -> p q t", t=38)
                for k in range(ND, 31):
                    ins = e.matmul(pg(gy, NPR, NSM), diag[ui][:, k, :], u16s[:, :, k:k + 8], start=(k == ND),
                                   stop=(k == 30))
                return ins
            P.add("pe", conv_s, reads=[("ub16", ui), ("diag", ui)], writes=[("pss", gy[1])])
            P.add("dve", both(lambda e, c=c, gy=gy: e.tensor_tensor(out=W(y_slot(c)), in0=pg(gy), in1=W(y_slot(c)),
                                                                    op=ALU.add)),
                  reads=pt(gy) + tk_big(2 * c, 2 * c + 1), writes=tk_big(2 * c, 2 * c + 1))
            ln_feed(c)

        stage1(0)
        for c in range(1, KC):
            stage1(c)
            if c == KC - 1:
                preswitch()
            stage2(c - 1)
        stage2(KC - 1)
        ln_flush()
        P.add("act", lambda e: e.activation(out=S[1][:], in_=S[2][:], func=AF.Square, bias=ZB, scale=1.0),
              reads=[("S", 2), "cst"], writes=[("S", 1)])
        P.add("dve", lambda e: e.tensor_tensor(out=S[0][:], in0=S[0][:], in1=S[1][:], op=ALU.subtract),
              reads=[("S", 0), ("S", 1)], writes=[("S", 0)])
        P.add("act", lambda e: e.activation(out=S[1][:], in_=S[0][:], func=AF.Ln, bias=cst[:, 2:3], scale=1.0),
              reads=[("S", 0), "cst"], writes=[("S", 1)])
        P.add("act", lambda e: e.activation(out=S[1][:], in_=S[1][:], func=AF.Exp, bias=ZB, scale=-0.5),
              reads=[("S", 1), "cst"], writes=[("S", 1)])
        postswitch()

        def ln_sub(c):
            yj = y_slot(c)
            P.add("dve", lambda e, yj=yj: e.tensor_tensor(out=yj, in0=yj, in1=S[2][:], op=ALU.subtract),
                  reads=[("S", 2)] + tk_big(2 * c, 2 * c + 1), writes=tk_big(2 * c, 2 * c + 1))
        ln_sub(0)
        ln_sub(1)
        for c in range(KC):
            yj = y_slot(c)
            P.add("dve", lambda e, yj=yj: e.tensor_tensor(out=yj, in0=yj, in1=S[1][:], op=ALU.mult),
                  reads=[("S", 1)] + tk_big(2 * c, 2 * c + 1), writes=tk_big(2 * c, 2 * c + 1))
            if c + 2 < KC:
                ln_sub(c + 2)
            P.add("act", lambda e, c=c, yj=yj: e.activation(out=xn[:, c, :], in_=yj, func=AF.Silu,
                                                            bias=vec(V_LNB + j, c), scale=vec(V_LNG + j, c)),
                  reads=tk_big(2 * c, 2 * c + 1) + ["vecs"], writes=[("xn", c)])
        out_proj(s, l, xn_rhs, lambda k: [("xn", k)], bias_idx=V_BPW2 + j)
        dsts = [(30 + 8 * q, 38 + 8 * q, sbs[j, s * NSQ + q, 22:30, :]) for q in range(NSQ)]
        if s == n_st - 1:
            dsts.append((0, 30, sbp[j, :, :]))
        state_out(s, 94, dsts)

    def ffn(s, l):
        rmsnorm(V_GFFN + l)

        def evac(f, gg, gu):
            ti = next_T()
            P.add("act", both(lambda e, gg=gg, ti=ti: e.activation(out=W(T[ti][:]), in_=pg(gg), func=AF.Silu, bias=ZB,
                                                              scale=1.0)),
                  reads=pt(gg), writes=[("T", ti)])
            P.add("dve", both(lambda e, f=f, gu=gu, ti=ti: e.tensor_tensor(out=W(act_slot(f)), in0=pg(gu), in1=W(T[ti][:]),
                                                                      op=ALU.mult)),
                  reads=pt(gu) + [("T", ti)], writes=tk_big(f))

        sg0 = xn_spec(s, ("fg", l, 0), 0)
        su0 = xn_spec(s, ("fu", l, 0), 0)
        sg1 = xn_spec(s, ("fg", l, 0), 128)
        em = mm_multi([sg0, su0, sg1])
        em[0]()
        em[1]()
        evac(0, sg0["g"], su0["g"])
        em[2]()
        gu1 = G()
        mm_group(gu1, range(KC), lambda k: w_lhsT(s, ("fu", l, 0), k, 128), xn_rhs, lambda k: [("xn", k)],
                 w_tokens(s, ("fu", l, 0)))
        wfree(s, ("fg", l, 0))
        wfree(s, ("fu", l, 0))
        evac(1, sg1["g"], gu1)
        load_p(l, s)
        for f in range(2, FC):
            fp = f // 2
            co = (f % 2) * 128
            gg = G()
            mm_group(gg, range(KC), lambda k: w_lhsT(s, ("fg", l, fp), k, co), xn_rhs, lambda k: [("xn", k)],
                     w_tokens(s, ("fg", l, fp)))
            gu = G()
            mm_group(gu, range(KC), lambda k: w_lhsT(s, ("fu", l, fp), k, co), xn_rhs, lambda k: [("xn", k)],
                     w_tokens(s, ("fu", l, fp)))
            if f % 2 == 1:
                wfree(s, ("fg", l, fp))
                wfree(s, ("fu", l, fp))
            evac(f, gg, gu)
        nstate["pm"] = nstate["on"]
        for n in range(KC):
            key = ("fd", l, n)
            g = G()
            mm_group(g, range(FC), lambda k, key=key: w_lhsT(s, key, k, 0), lambda k: act_slot(k),
                     lambda k: tk_big(k), w_tokens(s, key))
            wfree(s, key)
            resid_add(n, g)

    def ple(s, l):
        rmsnorm(V_GPLE + l)
        nstate["ps"] = KC - 1

        def evac(n, gg, gp):
            t1 = next_T()
            t2 = next_T()
            P.add("act", both(lambda e, gg=gg, t1=t1: e.activation(out=W(T[t1][:]), in_=pg(gg), func=AF.Tanh, bias=ZB,
                                                              scale=0.5)),
                  reads=pt(gg), writes=[("T", t1)])
            P.add("act", both(lambda e, gp=gp, t2=t2: e.activation(out=W(T[t2][:]), in_=pg(gp), func=AF.Identity, bias=ZB,
                                                              scale=0.5)),
                  reads=pt(gp), writes=[("T", t2)])
            P.add("dve", lambda e, t1=t1, t2=t2: e.scalar_tensor_tensor(
                out=T[t2][:], in0=T[t1][:], scalar=1.0, in1=T[t2][:], op0=ALU.add, op1=ALU.mult),
                reads=[("T", t1), ("T", t2)], writes=[("T", t2)])
            P.add("dve", lambda e, n=n, t2=t2: e.tensor_tensor(out=h[:, n, :], in0=h[:, n, :], in1=T[t2][:],
                                                               op=ALU.add),
                reads=[("T", t2), ("h", n)], writes=[("h", n)])
            if n > 0:
                norm_feed(n - 1)
            if n == KC - 1:
                norm_feed(n)

        def proj_spec(n):
            return dict(g=G(), klist=range(2), lhsT_fn=lambda k: w_lhsT(s, ("pp", l, 0), k, n * 128),
                        rhs_fn=lambda k: pT[:, k, :], reads_fn=lambda k: ["pT"], wtok=w_tokens(s, ("pp", l, 0)))

        sg0 = xn_spec(s, ("pg", l, 0), 0)
        sp0 = proj_spec(0)
        sg1 = xn_spec(s, ("pg", l, 0), 128)
        em = mm_multi([sp0, sg0, sg1])
        em[0]()
        em[1]()
        evac(0, sg0["g"], sp0["g"])
        em[2]()
        wfree(s, ("pg", l, 0))
        sp1 = proj_spec(1)
        mm_group(sp1["g"], sp1["klist"], sp1["lhsT_fn"], sp1["rhs_fn"], sp1["reads_fn"], sp1["wtok"])
        evac(1, sg1["g"], sp1["g"])
        if l + 1 < depth:
            load_state(s, l + 1)
        for n in range(2, KC):
            cp = n // 2
            co = (n % 2) * 128
            gg = G()
            mm_group(gg, range(KC), lambda k: w_lhsT(s, ("pg", l, cp), k, co), xn_rhs, lambda k: [("xn", k)],
                     w_tokens(s, ("pg", l, cp)))
            if n % 2 == 1:
                wfree(s, ("pg", l, cp))
            gp = G()
            mm_group(gp, range(2), lambda k: w_lhsT(s, ("pp", l, 0), k, n * 128), lambda k: pT[:, k, :],
                     lambda k: ["pT"], w_tokens(s, ("pp", l, 0)))
            if n == KC - 1:
                wfree(s, ("pp", l, 0))
            evac(n, gg, gp)
        nstate["ps"] = 4

    def final(s):
        P.add("sp", lambda e: e.dma_start(out=gfin, in_=gfin_in[:, :]),
              writes=tk_big(0, 1), dma="c1")
        tiles = [(yp[s * NPR + i * 128: s * NPR + (i + 1) * 128, :], 128, i * 128) for i in range(8)]
        tiles.append((ys[s * NSM:(s + 1) * NSM, :], NSM, NPR))
        P.add("dve", lambda e: e.memset(ssq[:], 0.0), writes=[("ssq", c_) for c_ in range(16)])
        for ti_, (dst, R, col0) in enumerate(tiles):
            g = G(False)

            def fn(e, g=g, R=R, col0=col0):
                ins = None
                for c in range(KC):
                    ins = e.transpose(pg(g, c * 128, 128)[0:R, :], h[:, c, col0:col0 + R], ident[:, :])
                return ins
            P.add("pe", fn, reads=[("h", c) for c in range(KC)] + ["ident"], writes=pt(g))
            si = ti_ % 4
            col = ti_
            fst = xn32[:, si * D:(si + 1) * D]
            ftok = [("fstg", si)]
            xtoks = [("xn", c) for c in range(KC)]


            def sqsum(e, g=g, R=R, si=si, col=col):
                return e.activation(out=T[0][0:R, 0:1024], in_=pg(g, 0, 1024)[0:R, :], func=AF.Square,
                                    bias=cst[0:R, 0:1], scale=1.0, accum_out=ssq[0:R, col:col + 1])
            P.add("act", sqsum, reads=pt(g) + [("ssq", col), "cst"], writes=[("T", 0), ("ssq", col)])
            P.add("act", lambda e, R=R, col=col: e.activation(out=ssq[0:R, col:col + 1], in_=ssq[0:R, col:col + 1],
                                                              func=AF.Ln, bias=cst[0:R, 1:2], scale=1.0 / D),
                  reads=[("ssq", col), "cst"], writes=[("ssq", col)])
            P.add("act", lambda e, R=R, col=col: e.activation(out=ssq[0:R, col:col + 1], in_=ssq[0:R, col:col + 1],
                                                              func=AF.Exp, bias=cst[0:R, 0:1], scale=-0.5),
                  reads=[("ssq", col), "cst"], writes=[("ssq", col)])
            P.add("dve", lambda e, g=g, R=R, fst=fst, col=col: e.scalar_tensor_tensor(
                out=fst[0:R, :], in0=pg(g, 0, 1024)[0:R, :], scalar=ssq[0:R, col:col + 1], in1=gfin[0:R, :],
                op0=ALU.mult, op1=ALU.mult),
                reads=pt(g) + [("ssq", col)] + tk_big(0, 1), writes=ftok + (xtoks if ti_ == 0 else []))
            P.add("sp", lambda e, dst=dst, R=R, fst=fst: e.dma_start(out=dst, in_=fst[0:R, :]),
                  reads=ftok + xtoks, dma="fo%d" % si)

    prefetch_x(0)
    ring["gate"] = [("xst", 4)]
    prefetch()
    for s in range(n_st):
        nstate["on"] = True
        load_x(s)
        nstate["pm"] = True
        for c in range(KC):
            norm_feed(c)
        load_state(s, 0)
        for l in range(depth):
            if l % 2 == 0:
                mixer_A(s, l)
            else:
                mixer_B(s, l)
            ffn(s, l)
            nstate["on"] = (l < depth - 1)
            if l == depth - 1 and s + 1 < n_st:
                prefetch_x(s + 1)
            ple(s, l)
        final(s)
    assert ring["next"] == NP_, (ring["next"], NP_)
    P.emit(stack)
    stack.close()
    return nc


def make_in_maps(inputs):
    wall = pack_weights(inputs)
    vecs = pack_vecs(inputs)
    gfin = np.ascontiguousarray(np.broadcast_to(np.asarray(inputs["g_final"], np.float32).reshape(1, D), (128, D)))
    maps = []
    for c in range(NCORE):
        sl = slice(16 * c, 16 * (c + 1))
        maps.append({
            "xp": np.ascontiguousarray(inputs["x_prompt"][c]),
            "xs": np.ascontiguousarray(np.asarray(inputs["x_sample"][sl]).reshape(128, D)),
            "pp": np.ascontiguousarray(inputs["p_prompt"][:, c]),
            "ps": np.ascontiguousarray(np.asarray(inputs["p_sample"][:, sl]).reshape(DEPTH, 128, PLE)),
            "sa": np.ascontiguousarray(inputs["state_conv_a"][:, sl]),
            "sb": np.ascontiguousarray(inputs["state_conv_b"][:, sl]),
            "vecs_in": vecs,
            "gfin_in": gfin,
            "wall": wall,
        })
    return maps


def gather(results):
    yp = np.stack([r["yp"] for r in results])
    ys = np.concatenate([r["ys"].reshape(16, 8, D) for r in results], axis=0)
    sap = np.stack([r["sap"] for r in results], axis=1)
    sas = np.concatenate([r["sas"] for r in results], axis=1)
    sbp = np.stack([r["sbp"] for r in results], axis=1)
    sbs = np.concatenate([r["sbs"] for r in results], axis=1)
    return tuple(np.ascontiguousarray(a, dtype=np.float32) for a in (yp, ys, sap, sas, sbp, sbs))


def kernel(**inputs):
    inputs = {k: np.asarray(v) for k, v in inputs.items()}
    nc = build_program()
    res = run_bass_kernel_spmd(nc, make_in_maps(inputs), core_ids=list(range(NCORE)))
    return gather(res.results)
```
